# Optimizing a Trainium2 kernel written in Bass

```python
import jax, jax.numpy as jnp
from jax import lax
import numpy as np

D_MODEL = 1024
BATCH = 2
SEQ = 16384
DEPTH = 2
DEC_BATCH = 8
DEC_SEQ = 8192
PAST_LEN = 128

N_MIXERS = 2
N_FNET_LAYERS = (DEPTH + 1) // 2
N_CONV_LAYERS = DEPTH // 2
D_MIX = D_MODEL
FNET_GROUPS = 8
FNET_GROUP_DIM = D_MIX // FNET_GROUPS
CONV_WIDTH = 3
D_FF = 2816
N_FFN_PER_LAYER = 2
N_LN_PER_LAYER = 3
ALPHA = float((2 * DEPTH) ** 0.25)
BETA = float((8 * DEPTH) ** -0.25)
LN_EPS = 1e-5

kernel_name = "hybrid_fnet_shortconv_macaron_encoder"


def layer_norm(x, g, b):
    xf = x.astype(jnp.float32)
    mu = jnp.mean(xf, axis=-1, keepdims=True)
    var = jnp.mean(jnp.square(xf - mu), axis=-1, keepdims=True)
    y = (xf - mu) * lax.rsqrt(var + LN_EPS) * g.astype(jnp.float32) + b.astype(jnp.float32)
    return y.astype(x.dtype)


def swiglu_ffn(x, w_gate, w_up, w_down):
    h = jax.nn.silu(jnp.einsum("bsd,df->bsf", x, w_gate)) * jnp.einsum("bsd,df->bsf", x, w_up)
    return jnp.einsum("bsf,fd->bsd", h, w_down)


def fourier_mixer(x, w_in, w_out):
    bsz, s, _ = x.shape
    u = jnp.einsum("bsd,de->bse", x, w_in).reshape(bsz, s, FNET_GROUPS, FNET_GROUP_DIM)
    f = jnp.fft.fft2(u.astype(jnp.float32), axes=(1, 3), norm="ortho").real
    f = f.reshape(bsz, s, D_MIX).astype(x.dtype)
    return jnp.einsum("bse,ed->bsd", f, w_out)


def centred_depthwise_conv(u, w):
    up = jnp.pad(u, ((0, 0), (1, 1), (0, 0)))
    return up[:, :-2] * w[0] + up[:, 1:-1] * w[1] + up[:, 2:] * w[2]


def short_conv_mixer(x, w_in, w_conv, w_out):
    proj = jnp.einsum("bsd,de->bse", x, w_in)
    gate_b, gate_c, h = jnp.split(proj, 3, axis=-1)
    y = gate_b * centred_depthwise_conv(gate_c * h, w_conv)
    return jnp.einsum("bse,ed->bsd", y, w_out)


def trunk(x, ffn_w_gate, ffn_w_up, ffn_w_down, ln_g, ln_b,
          fnet_w_in, fnet_w_out, conv_w_in, conv_w, conv_w_out):
    for i in range(DEPTH):
        x = layer_norm(ALPHA * x + 0.5 * swiglu_ffn(x, ffn_w_gate[i, 0], ffn_w_up[i, 0], ffn_w_down[i, 0]),
                       ln_g[i, 0], ln_b[i, 0])
        j = i // N_MIXERS
        if i % N_MIXERS == 0:
            m = fourier_mixer(x, fnet_w_in[j], fnet_w_out[j])
        else:
            m = short_conv_mixer(x, conv_w_in[j], conv_w[j], conv_w_out[j])
        x = layer_norm(ALPHA * x + m, ln_g[i, 1], ln_b[i, 1])
        x = layer_norm(ALPHA * x + 0.5 * swiglu_ffn(x, ffn_w_gate[i, 1], ffn_w_up[i, 1], ffn_w_down[i, 1]),
                       ln_g[i, 2], ln_b[i, 2])
    return x


def setup_inputs(seed: int = 0) -> dict:
    key = jax.random.key(seed)
    ks = jax.random.split(key, 13)
    f32 = jnp.float32
    d_in = D_MODEL ** -0.5
    return {
        "x_prompt": jax.random.normal(ks[0], (BATCH, SEQ, D_MODEL), f32),
        "x_sample": jax.random.normal(ks[1], (DEC_BATCH, DEC_SEQ, D_MODEL), f32),
        "ffn_w_gate": jax.random.normal(ks[2], (DEPTH, N_FFN_PER_LAYER, D_MODEL, D_FF), f32) * d_in,
        "ffn_w_up": jax.random.normal(ks[3], (DEPTH, N_FFN_PER_LAYER, D_MODEL, D_FF), f32) * d_in,
        "ffn_w_down": jax.random.normal(ks[4], (DEPTH, N_FFN_PER_LAYER, D_FF, D_MODEL), f32) * (D_FF ** -0.5 * BETA),
        "ln_g": 1.0 + 0.02 * jax.random.normal(ks[5], (DEPTH, N_LN_PER_LAYER, D_MODEL), f32),
        "ln_b": 0.02 * jax.random.normal(ks[6], (DEPTH, N_LN_PER_LAYER, D_MODEL), f32),
        "fnet_w_in": jax.random.normal(ks[7], (N_FNET_LAYERS, D_MODEL, D_MIX), f32) * d_in,
        "fnet_w_out": jax.random.normal(ks[8], (N_FNET_LAYERS, D_MIX, D_MODEL), f32) * (D_MIX ** -0.5 * BETA),
        "conv_w_in": jax.random.normal(ks[9], (N_CONV_LAYERS, D_MODEL, 3 * D_MIX), f32) * d_in,
        "conv_w": jax.random.normal(ks[10], (N_CONV_LAYERS, CONV_WIDTH, D_MIX), f32) * (CONV_WIDTH ** -0.5),
        "conv_w_out": jax.random.normal(ks[11], (N_CONV_LAYERS, D_MIX, D_MODEL), f32) * (D_MIX ** -0.5 * BETA),
    }


def reference(x_prompt, x_sample, ffn_w_gate, ffn_w_up, ffn_w_down, ln_g, ln_b,
              fnet_w_in, fnet_w_out, conv_w_in, conv_w, conv_w_out):
    y_prompt = trunk(x_prompt, ffn_w_gate, ffn_w_up, ffn_w_down, ln_g, ln_b,
                     fnet_w_in, fnet_w_out, conv_w_in, conv_w, conv_w_out)
    y_sample = trunk(x_sample, ffn_w_gate, ffn_w_up, ffn_w_down, ln_g, ln_b,
                     fnet_w_in, fnet_w_out, conv_w_in, conv_w, conv_w_out)
    return (y_prompt, y_sample)
```

```python
import numpy as np
import ml_dtypes
from contextlib import ExitStack
import concourse.bass as bass
import concourse.mybir as mybir
from concourse.bass_utils import run_bass_kernel_spmd

F32 = mybir.dt.float32
BF16 = mybir.dt.bfloat16
AF = mybir.ActivationFunctionType
ALU = mybir.AluOpType
bf16 = ml_dtypes.bfloat16

D = 1024
DFF = 2816
NDC = 8
NFC = 22
T = 512
NT = 32
NTOK = NT * T
ALPHA = 2.0 ** 0.5
EPS = 1e-5
NGU = 112
ND = 32
GU_WIN, GU_WOUT, GU_CIN, GU_COUT = 88, 92, 96, 108
NPAIR = 18
DBG_STOP = 99
PREV_PAIRS = [(0, 'pt', 2), (0, 'pt', 3), (1, 'pt', 2), (1, 'pt', 3), (1, 'own', 0),
              (2, 'own', 0), (2, 'own', 1), (3, 'own', 1), (3, 'own', 2)]
NEXT_PAIRS = [(0, 'own', 1), (0, 'own', 2), (1, 'own', 2), (1, 'own', 3), (2, 'own', 3),
              (2, 'nt', 0), (2, 'nt', 1), (3, 'nt', 0), (3, 'nt', 1)]


def gu_ffn(l, k):
    return (l * 2 + k) * 22


def d_ffn(l, k):
    return (l * 2 + k) * 8


class Buf:
    __slots__ = ("ap", "w", "r", "const")

    def __init__(self, ap, const=False):
        self.ap = ap
        self.w = None
        self.r = {}
        self.const = const


class Sched:
    KD = 12
    KDQ = {"sp": 12, "pool": 2}

    def __init__(self, nc, stack, dry=False):
        self.nc = nc
        self.dry = dry
        self.eng = {"pe": nc.tensor, "act": nc.scalar, "dve": nc.vector,
                    "pool": nc.gpsimd, "sp": nc.sync}
        self.csem = {}
        self.ccnt = {}
        self.waited = {e: {} for e in self.eng}
        self.dsem = {}
        self.dcnt = {}
        self.dnext = {}
        if not dry:
            for e in ("pe", "act", "dve", "pool"):
                self.csem[e] = stack.enter_context(nc.semaphore("c_" + e))
                self.ccnt[e] = 0
            for q in ("sp", "pool"):
                self.dsem[q] = [stack.enter_context(nc.semaphore("d_%s%d" % (q, i)))
                                for i in range(self.KDQ[q])]
                self.dcnt[q] = [0] * self.KDQ[q]
                self.dnext[q] = 0

    def _wait_key(self, e, key, raw):
        kind, owner, sem, val = key
        if kind == "c" and owner == e:
            if e == "pe" or not raw:
                return
        w = self.waited[e]
        if w.get(id(sem), 0) >= val:
            return
        self.eng[e].wait_ge(sem, val)
        w[id(sem)] = val

    def _deps(self, e, reads, writes):
        for b in reads:
            if b.w is not None:
                self._wait_key(e, b.w, True)
        for b in writes:
            if b.w is not None:
                self._wait_key(e, b.w, True)
            for k in b.r.values():
                self._wait_key(e, k, False)

    def _record(self, key, reads, writes):
        sid = id(key[2])
        for b in reads:
            if not b.const:
                b.r[sid] = key
        for b in writes:
            b.w = key
            b.r = {}

    def op(self, e, fn, reads, writes, inc=True):
        if self.dry:
            return
        self._deps(e, reads, writes)
        ins = fn(self.eng[e])
        if e == "pe" and not inc:
            key = ("c", "pe", self.csem["pe"], self.ccnt["pe"] + 1)
        else:
            self.ccnt[e] += 1
            ins.then_inc(self.csem[e], 1)
            key = ("c", e, self.csem[e], self.ccnt[e])
        self._record(key, reads, writes)

    def dma(self, q, out_ap, in_ap, reads, writes):
        if self.dry:
            return
        self._deps(q, reads, writes)
        k = self.dnext[q]
        self.dnext[q] = (k + 1) % self.KDQ[q]
        sem = self.dsem[q][k]
        if self.dcnt[q][k] > 0:
            w = self.waited[q]
            if w.get(id(sem), 0) < self.dcnt[q][k]:
                self.eng[q].wait_ge(sem, self.dcnt[q][k])
                w[id(sem)] = self.dcnt[q][k]
        self.dcnt[q][k] += 16
        self.eng[q].dma_start(out=out_ap, in_=in_ap).then_inc(sem, 16)
        key = ("d", q, sem, self.dcnt[q][k])
        self._record(key, reads, writes)

    def barrier(self, engines=("sp", "pool")):
        if self.dry:
            return
        for e in engines:
            for q in ("sp", "pool"):
                for k in range(self.KDQ[q]):
                    v = self.dcnt[q][k]
                    if v > 0 and self.waited[e].get(id(self.dsem[q][k]), 0) < v:
                        self.eng[e].wait_ge(self.dsem[q][k], v)
                        self.waited[e][id(self.dsem[q][k])] = v

    def mm(self, out_buf, out_ap, l_buf, l_ap, r_buf, r_ap, start, stop):
        self.op("pe", lambda e: e.matmul(out_ap, l_ap, r_ap, start=start, stop=stop),
                [l_buf, r_buf], [out_buf], inc=stop)


class Rot:
    def __init__(self, bufs):
        self.bufs = bufs
        self.i = 0

    def next(self):
        b = self.bufs[self.i % len(self.bufs)]
        self.i += 1
        return b


class WRing:
    def __init__(self, S, slots, dram, seq):
        self.S = S
        self.slots = slots
        self.dram = dram
        self.seq = seq
        self.pos = 0
        self.issued = 0
        self.consumed = 0
        self.castbufs = None

    def _prefetch(self):
        S = self.S
        R = len(self.slots)
        while self.issued < len(self.seq) and self.issued < self.consumed + R:
            n = self.issued
            sl = self.slots[n % R]
            S.dma("sp", sl.ap, self.dram[self.seq[n]], [self.castbufs[self.seq[n] // 4]], [sl])
            self.issued += 1

    def use(self, pid):
        S = self.S
        if S.dry:
            self.seq.append(pid)
            return self.slots[0]
        assert self.seq[self.pos] == pid, (self.pos, self.seq[self.pos], pid)
        self._prefetch()
        assert self.pos < self.issued, "weight ring: too many pieces held"
        sl = self.slots[self.pos % len(self.slots)]
        self.pos += 1
        return sl

    def done(self, k=1):
        if self.S.dry:
            return
        self.consumed += k
        assert self.consumed <= self.pos
        self._prefetch()


def build_program(nt_run=NT, phases=("p0", "p1", "p3a", "p3b"), dbg=False):
    nc = bass.Bass("TRN2", target_bir_lowering=False)
    dt = nc.dram_tensor
    xT = dt("xT", [D, NTOK], F32, kind="ExternalInput").ap()
    wgu = dt("wgu", [NGU, 128, 2048], F32, kind="ExternalInput").ap()
    wd = dt("wd", [ND, 128, 2816], F32, kind="ExternalInput").ap()
    m1tab = dt("m1tab", [128, 128 * 384], BF16, kind="ExternalInput").ap()
    m2tab = dt("m2tab", [128, 256], BF16, kind="ExternalInput").ap()
    cstab = dt("cstab", [128, 256], BF16, kind="ExternalInput").ap()
    ptab = dt("ptab", [128, NT * NPAIR * 128], BF16, kind="ExternalInput").ap()
    identd = dt("ident", [128, 128], BF16, kind="ExternalInput").ap()
    bwd = dt("bw", [128, 2], F32, kind="ExternalInput").ap()
    lngd = dt("lng", [128, 48], F32, kind="ExternalInput").ap()
    lnbd = dt("lnb", [128, 48], F32, kind="ExternalInput").ap()
    cwbd = dt("cwb", [128, 3072], F32, kind="ExternalInput").ap()
    outT = dt("outT", [D, NTOK], F32, kind="ExternalOutput").ap()
    skind = "ExternalOutput" if dbg else "Internal"
    wgu_b = dt("wgu_b", [NGU, 128, 2048], BF16, kind="Internal").ap()
    wd_b = dt("wd_b", [ND, 128, 2816], BF16, kind="Internal").ap()
    x1T = dt("x1T", [D, NTOK], F32, kind=skind).ap()
    ysc = dt("ysc", [128, 128, 2048], BF16, kind="Internal").ap()
    x4T = dt("x4T", [D, NTOK], F32, kind=skind).ap()
    gbT = dt("gbT", [D, NTOK], BF16, kind="Internal").ap()
    vsc = dt("vsc", [3, 128, 128, 1024], BF16, kind="Internal").ap()

    seqs = {"gu": [], "d": []}
    for dry in (True, False):
        with ExitStack() as st:
            S = Sched(nc, st, dry=dry)
            _emit(nc, st, S, seqs, nt_run, phases, locals())
    return nc


def _emit(nc, st, S, seqs, nt_run, phases, g):
    xT, wgu, wd, m1tab, m2tab, cstab, ptab = g["xT"], g["wgu"], g["wd"], g["m1tab"], g["m2tab"], g["cstab"], g["ptab"]
    identd, bwd, lngd, lnbd, cwbd, outT = g["identd"], g["bwd"], g["lngd"], g["lnbd"], g["cwbd"], g["outT"]
    wgu_b, wd_b, x1T, ysc, x4T, gbT, vsc = g["wgu_b"], g["wd_b"], g["x1T"], g["ysc"], g["x4T"], g["gbT"], g["vsc"]
    dry = S.dry

    def sb(name, cols, dtp, stk=st):
        return stk.enter_context(nc.sbuf_tensor("s_" + name + ("_d" if dry else ""), [128, cols], dtp))

    def chunks(t, n, w):
        return [Buf(t[:, i * w:(i + 1) * w]) for i in range(n)]

    psum = [st.enter_context(nc.psum_tensor("ps%d%s" % (i, "_d" if dry else ""), [128, 512], F32))
            for i in range(8)]
    PS = Rot([Buf(psum[i][:, :]) for i in range(6)])
    S1 = Buf(psum[6][:, :])
    S2 = Buf(psum[7][:, :])
    ones_t = sb("ones", 128, BF16)
    ones = Buf(ones_t[:, :], const=True)
    cs_t = sb("cs", 256, BF16)
    cs = Buf(cs_t[:, :], const=True)
    m2_t = sb("m2", 256, BF16)
    m2 = Buf(m2_t[:, :], const=True)
    id_t = sb("ident", 128, BF16)
    ident = Buf(id_t[:, :], const=True)
    bw_t = sb("bw", 2, F32)
    bw = Buf(bw_t[:, :], const=True)
    lng_t = sb("lng", 48, F32)
    lng = Buf(lng_t[:, :], const=True)
    lnb_t = sb("lnb", 48, F32)
    lnb = Buf(lnb_t[:, :], const=True)
    X_t = sb("X", NDC * T, F32)
    X = chunks(X_t, NDC, T)
    xb_t = sb("xb", NDC * T, BF16)
    xb = chunks(xb_t, NDC, T)
    hT_t = sb("hT", NFC * T, BF16)
    hT = chunks(hT_t, NFC, T)
    sq_t = sb("sq", 3 * T, BF16)
    SQ = Rot(chunks(sq_t, 3, T))
    rb_t = sb("rb", 3 * T, BF16)
    RB = Rot(chunks(rb_t, 3, T))
    sg_t = sb("sg", 2 * T, F32)
    SG = Rot(chunks(sg_t, 2, T))
    mean, var, rstd = [Buf(sb("stat%d" % i, T, F32)[:, :]) for i in range(3)]
    NGUS, NDS = 6, 4
    gus_t = [sb("gus%d" % i, 2048, BF16) for i in range(NGUS)]
    ds_t = [sb("ds%d" % i, 2816, BF16) for i in range(NDS)]
    GU = WRing(S, [Buf(t[:, :]) for t in gus_t], wgu_b, seqs["gu"])
    DR = WRing(S, [Buf(t[:, :]) for t in ds_t], wd_b, seqs["d"])

    S.op("dve", lambda e: e.memset(ones_t[:, :], 1.0), [], [ones])
    dmy_t = sb("dmy", 2, F32)
    dmy = Buf(dmy_t[:, :])
    S.op("dve", lambda e: e.memset(dmy_t[:, :], 1.0), [], [dmy])
    epsc_t = sb("epsc", 2, F32)
    epsc = Buf(epsc_t[:, :], const=True)
    S.op("dve", lambda e: e.memset(epsc_t[:, 0:1], EPS), [], [epsc])
    S.op("dve", lambda e: e.memset(epsc_t[:, 1:2], 4.0 * EPS), [], [epsc])
    S.dma("sp", cs_t[:, :], cstab, [], [cs])
    S.dma("sp", m2_t[:, :], m2tab, [], [m2])
    S.dma("sp", id_t[:, :], identd, [], [ident])
    S.dma("sp", bw_t[:, :], bwd, [], [bw])
    S.dma("sp", lng_t[:, :], lngd, [], [lng])
    S.dma("sp", lnb_t[:, :], lnbd, [], [lnb])

    gu_cast = [Buf(None) for _ in range(NGU // 4)]
    d_cast = [Buf(None) for _ in range(ND // 4)]
    GU.castbufs = gu_cast
    DR.castbufs = d_cast
    first = [("gu", g_) for g_ in range(6)] + [("d", 0), ("d", 1), ("gu", GU_WIN // 4)]
    rest = [("gu", g_) for g_ in range(6, NGU // 4) if g_ != GU_WIN // 4] + [("d", g_) for g_ in range(2, ND // 4)]
    if nt_run < 16:
        first, rest = first + rest, []

    def cast_group(kind, g_):
        if kind == "gu":
            S.dma("pool", wgu_b[4 * g_:4 * g_ + 4], wgu[4 * g_:4 * g_ + 4], [], [gu_cast[g_]])
        else:
            S.dma("pool", wd_b[4 * g_:4 * g_ + 4], wd[4 * g_:4 * g_ + 4], [], [d_cast[g_]])
    if "p0" in phases:
        for kind, g_ in first:
            cast_group(kind, g_)
        if "p1" not in phases:
            for kind, g_ in rest:
                cast_group(kind, g_)
            rest = []

    def ffn(l, k):
        gb_, db_ = gu_ffn(l, k), d_ffn(l, k)

        def evac(fc, pg, pu):
            sg = SG.next()
            S.op("act", lambda e, o=sg.ap, i=pg.ap: e.activation(out=o, in_=i, func=AF.Silu), [pg], [sg])
            S.op("dve", lambda e, o=hT[fc].ap, a=sg.ap, b=pu.ap: e.tensor_tensor(out=o, in0=a, in1=b, op=ALU.mult),
                 [sg, pu], [hT[fc]])

        def wsl(w, dc, cc):
            return w.ap[:, dc * 256 + cc * 128: dc * 256 + cc * 128 + 128]
        w = [GU.use(gb_ + i) for i in range(4)]
        banks = [(PS.next(), PS.next()) for _ in range(3)]
        for dc in range(NDC):
            for fc in range(3):
                pg, pu = banks[fc]
                S.mm(pg, pg.ap, w[2 * (fc // 2)], wsl(w[2 * (fc // 2)], dc, fc % 2), xb[dc], xb[dc].ap, dc == 0, dc == NDC - 1)
                S.mm(pu, pu.ap, w[2 * (fc // 2) + 1], wsl(w[2 * (fc // 2) + 1], dc, fc % 2), xb[dc], xb[dc].ap, dc == 0, dc == NDC - 1)
        GU.done(2)
        for fc in range(3):
            evac(fc, *banks[fc])
        pg, pu = PS.next(), PS.next()
        for dc in range(NDC):
            S.mm(pg, pg.ap, w[2], wsl(w[2], dc, 1), xb[dc], xb[dc].ap, dc == 0, dc == NDC - 1)
        for dc in range(NDC):
            S.mm(pu, pu.ap, w[3], wsl(w[3], dc, 1), xb[dc], xb[dc].ap, dc == 0, dc == NDC - 1)
        GU.done(2)
        evac(3, pg, pu)
        for fp in range(2, 11):
            wg = GU.use(gb_ + 2 * fp)
            wu = GU.use(gb_ + 2 * fp + 1)
            for cc in range(2):
                fc = fp * 2 + cc
                pg, pu = PS.next(), PS.next()
                for dc in range(NDC):
                    S.mm(pg, pg.ap, wg, wsl(wg, dc, cc), xb[dc], xb[dc].ap, dc == 0, dc == NDC - 1)
                for dc in range(NDC):
                    S.mm(pu, pu.ap, wu, wsl(wu, dc, cc), xb[dc], xb[dc].ap, dc == 0, dc == NDC - 1)
                evac(fc, pg, pu)
            GU.done(2)
        return db_

    def resid_stats(c, src, scale, py):
        S.op("dve", lambda e, o=X[c].ap, a=src.ap, b=py.ap: e.scalar_tensor_tensor(
            out=o, in0=a, scalar=float(scale), in1=b, op0=ALU.mult, op1=ALU.add), [src, py], [X[c]])
        stats(c)

    pending = []

    def flush_stats():
        while pending:
            c, sq, rb = pending.pop(0)
            S.mm(S1, S1.ap, ones, ones.ap, rb, rb.ap, c == 0, c == NDC - 1)
            S.mm(S2, S2.ap, ones, ones.ap, sq, sq.ap, c == 0, c == NDC - 1)

    def stats(c):
        sq, rb = SQ.next(), RB.next()
        if c == 0:
            S.op("act", lambda e: e.activation(out=dmy_t[:, 0:1], in_=dmy_t[:, 1:2], func=AF.Ln), [dmy], [dmy])
        S.op("dve", lambda e, o=sq.ap, a=X[c].ap: e.tensor_tensor(out=o, in0=a, in1=a, op=ALU.mult), [X[c]], [sq])
        S.op("act", lambda e, o=rb.ap, a=X[c].ap: e.activation(out=o, in_=a, func=AF.Copy), [X[c]], [rb])
        pending.append((c, sq, rb))

    def ffn_down_ln(l, k, src, lnidx, finish=True):
        db_ = ffn(l, k)
        for c in range(NDC):
            wdd = DR.use(db_ + c)
            py = PS.next()
            for fc in range(NFC):
                S.mm(py, py.ap, wdd, wdd.ap[:, fc * 128:(fc + 1) * 128], hT[fc], hT[fc].ap, fc == 0, fc == NFC - 1)
            DR.done(1)
            flush_stats()
            resid_stats(c, src[c], 2.0 * ALPHA, py)
        if finish:
            ln_finish(lnidx, 4.0 * EPS)

    def ln_finish(lnidx, eps, out=None):
        flush_stats()
        eps_ap = epsc_t[:, 0:1] if eps == EPS else epsc_t[:, 1:2]
        S.op("dve", lambda e: e.tensor_scalar(out=mean.ap, in0=S1.ap, scalar1=1.0 / D, scalar2=None, op0=ALU.mult),
             [S1], [mean])
        S.op("dve", lambda e: e.tensor_tensor(out=var.ap, in0=mean.ap, in1=mean.ap, op=ALU.mult), [mean], [var])
        S.op("dve", lambda e: e.scalar_tensor_tensor(out=var.ap, in0=S2.ap, scalar=1.0 / D, in1=var.ap,
                                                     op0=ALU.mult, op1=ALU.subtract), [S2, var], [var])
        S.op("act", lambda e: e.activation(out=rstd.ap, in_=var.ap, func=AF.Ln, bias=eps_ap, scale=1.0),
             [var, epsc], [rstd])
        S.op("act", lambda e: e.activation(out=rstd.ap, in_=rstd.ap, func=AF.Exp, scale=-0.5), [rstd], [rstd])
        S.op("act", lambda e: e.activation(out=dmy_t[:, 0:1], in_=dmy_t[:, 1:2], func=AF.Silu), [dmy], [dmy])

        def sub(c):
            S.op("dve", lambda e, o=X[c].ap: e.tensor_tensor(out=o, in0=o, in1=mean.ap, op=ALU.subtract), [X[c], mean], [X[c]])

        def mul_cast(c):
            gc_ = lng_t[:, lnidx * 8 + c: lnidx * 8 + c + 1]
            bc_ = lnb_t[:, lnidx * 8 + c: lnidx * 8 + c + 1]
            S.op("dve", lambda e, o=X[c].ap: e.tensor_tensor(out=o, in0=o, in1=rstd.ap, op=ALU.mult), [X[c], rstd], [X[c]])
            if out is None:
                S.op("act", lambda e, o=xb[c].ap, a=X[c].ap, g_=gc_, b_=bc_: e.activation(
                    out=o, in_=a, func=AF.Identity, bias=b_, scale=g_), [X[c], lng, lnb], [xb[c]])
            else:
                S.op("act", lambda e, o=out[c].ap, a=X[c].ap, g_=gc_, b_=bc_: e.activation(
                    out=o, in_=a, func=AF.Identity, bias=b_, scale=g_), [X[c], lng, lnb], [out[c]])
        npool = 0 if lnidx == 1 else 3
        for c in range(NDC - npool, NDC):
            S.op("pool", lambda e, o=X[c].ap: e.tensor_tensor(out=o, in0=o, in1=mean.ap, op=ALU.subtract),
                 [X[c], mean], [X[c]])
        for c in range(3):
            sub(c)
        for c in range(3):
            mul_cast(c)
        for c in range(3, NDC - npool):
            sub(c)
            mul_cast(c)
        for c in range(NDC - npool, NDC):
            mul_cast(c)
        for c in range(NDC if out is None else 0):
            gc_ = lng_t[:, lnidx * 8 + c: lnidx * 8 + c + 1]
            bc_ = lnb_t[:, lnidx * 8 + c: lnidx * 8 + c + 1]
            S.op("act", lambda e, o=X[c].ap, g_=gc_, b_=bc_: e.activation(
                out=o, in_=o, func=AF.Identity, bias=b_, scale=g_), [X[c], lng, lnb], [X[c]])

    def proj_fm(pbase, npieces, rhs, consume):
        for q in range(npieces):
            w = GU.use(pbase + q)
            for cc in range(2):
                ec = q * 2 + cc
                p = PS.next()
                for dc in range(NDC):
                    S.mm(p, p.ap, w, w.ap[:, dc * 256 + cc * 128: dc * 256 + cc * 128 + 128],
                         rhs[dc], rhs[dc].ap, dc == 0, dc == NDC - 1)
                flush_stats()
                consume(ec, p)
            GU.done(1)

    def load_fm(dst_t, dst_bufs, src, j):
        S.dma("sp", dst_t[:, :].rearrange("p (c q) -> p c q", c=NDC),
              src[:, j * T:(j + 1) * T].rearrange("(c p) q -> p c q", p=128), [], dst_bufs)

    def store_fm(dst, src_t, src_bufs, j):
        S.dma("pool", dst[:, j * T:(j + 1) * T].rearrange("(c p) q -> p c q", p=128),
              src_t[:, :].rearrange("p (c q) -> p c q", c=NDC), src_bufs, [])

    if "p1" in phases:
        with ExitStack() as ps1:
            uT_t = sb("uT", NDC * T, BF16, ps1)
            uT = chunks(uT_t, NDC, T)
            zr_t = [sb("zr%d" % i, 1024, BF16, ps1) for i in range(2)]
            zi_t = [sb("zi%d" % i, 1024, BF16, ps1) for i in range(2)]
            ZR = Rot([Buf(t[:, :]) for t in zr_t])
            ZI = Rot([Buf(t[:, :]) for t in zi_t])
            yb_t = [sb("yb%d" % i, 2048, BF16, ps1) for i in range(2)]
            YB = Rot([Buf(t[:, :]) for t in yb_t])
            m1_t = [sb("m1t%d" % i, 4 * 384, BF16, ps1) for i in range(2)]
            M1 = Rot([Buf(t[:, :]) for t in m1_t])
            X2_t = sb("X2", NDC * T, F32, ps1)
            xsets = [(X_t, list(X)), (X2_t, chunks(X2_t, NDC, T))]
            load_fm(xsets[0][0], xsets[0][1], xT, 0)
            for j in range(nt_run):
                Xc_t, Xc = xsets[j % 2]
                X[:] = Xc
                if j + 1 < nt_run:
                    load_fm(xsets[(j + 1) % 2][0], xsets[(j + 1) % 2][1], xT, j + 1)
                m1 = M1.next()
                S.dma("sp", m1.ap, m1tab[:, j * 1536:(j + 1) * 1536], [], [m1])
                for c in range(NDC):
                    if c % 2 == 0:
                        S.op("act", lambda e, o=xb[c].ap, a=X[c].ap: e.activation(out=o, in_=a, func=AF.Copy), [X[c]], [xb[c]])
                    else:
                        S.op("dve", lambda e, o=xb[c].ap, a=X[c].ap: e.tensor_copy(out=o, in_=a), [X[c]], [xb[c]])
                if DBG_STOP <= 1:
                    continue
                ffn_down_ln(0, 0, X, 0)
                if DBG_STOP <= 2:
                    continue
                store_fm(x1T, Xc_t, X, j)
                if DBG_STOP <= 3:
                    continue

                def cons_u(ec, p):
                    if ec % 2 == 0:
                        S.op("act", lambda e, o=uT[ec].ap, a=p.ap: e.activation(out=o, in_=a, func=AF.Copy), [p], [uT[ec]])
                    else:
                        S.op("dve", lambda e, o=uT[ec].ap, a=p.ap: e.tensor_copy(out=o, in_=a), [p], [uT[ec]])
                proj_fm(GU_WIN, 4, xb, cons_u)
                if DBG_STOP <= 4:
                    continue
                def z_block(i):
                    zr, zi = ZR.next(), ZI.next()
                    for gp in range(4):
                        pz = PS.next()
                        for gg in range(2):
                            gidx = gp * 2 + gg
                            S.mm(pz, pz.ap[:, gg * 256:(gg + 1) * 256], uT[gidx], uT[gidx].ap[:, i * 128:(i + 1) * 128],
                                 cs, cs.ap, True, True)
                        for gg in range(2):
                            o0 = gp * 256 + gg * 128
                            for (dstb, lo) in ((zr, 0), (zi, 128)):
                                o_ap = dstb.ap[:, o0:o0 + 128]
                                a_ap = pz.ap[:, gg * 256 + lo:gg * 256 + lo + 128]
                                if gp % 2 == 0:
                                    S.op("act", lambda e, o=o_ap, a=a_ap: e.activation(out=o, in_=a, func=AF.Copy), [pz], [dstb])
                                else:
                                    S.op("dve", lambda e, o=o_ap, a=a_ap: e.tensor_copy(out=o, in_=a), [pz], [dstb])
                    return zr, zi

                def s1_block(i, zr, zi):
                    b0 = i * 384
                    m1r, m1i, m1n = m1.ap[:, b0:b0 + 128], m1.ap[:, b0 + 128:b0 + 256], m1.ap[:, b0 + 256:b0 + 384]
                    yb = YB.next()
                    for h in range(2):
                        pr, pi = PS.next(), PS.next()
                        S.mm(pr, pr.ap, m1, m1r, zr, zr.ap[:, h * 512:(h + 1) * 512], True, False)
                        S.mm(pr, pr.ap, m1, m1n, zi, zi.ap[:, h * 512:(h + 1) * 512], False, True)
                        S.mm(pi, pi.ap, m1, m1i, zr, zr.ap[:, h * 512:(h + 1) * 512], True, False)
                        S.mm(pi, pi.ap, m1, m1r, zi, zi.ap[:, h * 512:(h + 1) * 512], False, True)
                        S.op("act", lambda e, o=yb.ap[:, h * 512:(h + 1) * 512], a=pr.ap: e.activation(out=o, in_=a, func=AF.Copy), [pr], [yb])
                        S.op("dve", lambda e, o=yb.ap[:, 1024 + h * 512:1024 + (h + 1) * 512], a=pi.ap: e.tensor_copy(out=o, in_=a), [pi], [yb])
                    S.dma("pool", ysc[:, 4 * j + i, :], yb.ap, [yb], [])
                prev = z_block(0)
                for i in range(1, 4):
                    cur = z_block(i)
                    s1_block(i - 1, *prev)
                    prev = cur
                s1_block(3, *prev)
                for _ in range(2):
                    if rest:
                        cast_group(*rest.pop(0))
            while rest:
                cast_group(*rest.pop(0))
            X[:] = xsets[0][1]
        S.barrier()

    if "p3a" in phases:
        with ExitStack() as ps3:
            XB_t = sb("XB", NDC * T, F32, ps3)
            XB = chunks(XB_t, NDC, T)
            fT_t = sb("fT", NDC * T, BF16, ps3)
            fT = chunks(fT_t, NDC, T)
            yt_t = [sb("yt%d" % i, 2048, BF16, ps3) for i in range(4)]
            YTB = [Buf(t[:, :]) for t in yt_t]

            def load_yt(j):
                for i in range(4):
                    S.dma("sp", YTB[i].ap, ysc[4 * j + i, :, :], [], [YTB[i]])
            gbs_t = sb("gbs", NDC * T, BF16, ps3)
            GBS = chunks(gbs_t, NDC, T)
            gcs_t = sb("gcs", 2 * 256, F32, ps3)
            GCS = Rot(chunks(gcs_t, 2, 256))
            v32_t = sb("v32", 3 * 256, F32, ps3)
            V32 = Rot(chunks(v32_t, 3, 256))
            vst_t = [[sb("vst%d_%d" % (k, i), 1024, BF16, ps3) for i in range(4)] for k in range(3)]
            VST = [[Buf(vst_t[k][i][:, :]) for i in range(4)] for k in range(3)]
            cwb_t = sb("cwb", 3072, F32, ps3)
            cwb = Buf(cwb_t[:, :], const=True)
            S.dma("sp", cwb_t[:, :], cwbd, [], [cwb])

            def load_in(j):
                load_fm(XB_t, XB, x1T, j)
            load_in(0)
            load_yt(0)
            for j in range(nt_run):
                for c in range(NDC):
                    S.op("act", lambda e, o=X[c].ap, i=XB[c].ap: e.activation(out=o, in_=i, func=AF.Copy, scale=float(ALPHA)),
                         [XB[c]], [X[c]])
                    xo3 = X[c].ap.rearrange("p (a q) -> p a q", a=2)
                    xi3 = XB[c].ap.rearrange("p (a q) -> p a q", a=2)
                    S.op("dve", lambda e, o=xo3[:, :, 64:192], i=xi3[:, :, 64:192]: e.tensor_scalar(
                        out=o, in0=i, scalar1=bw_t[:, 0:1], scalar2=None, op0=ALU.mult), [XB[c], X[c], bw], [X[c]])
                    S.op("dve", lambda e, o=xo3[:, :, 64:128], i=xi3[:, :, 128:192]: e.scalar_tensor_tensor(
                        out=o, in0=i, scalar=bw_t[:, 1:2], in1=o, op0=ALU.mult, op1=ALU.add), [XB[c], X[c], bw], [X[c]])
                    S.op("dve", lambda e, o=xo3[:, :, 128:192], i=xi3[:, :, 64:128]: e.scalar_tensor_tensor(
                        out=o, in0=i, scalar=bw_t[:, 1:2], in1=o, op0=ALU.mult, op1=ALU.add), [XB[c], X[c], bw], [X[c]])
                if j + 1 < nt_run:
                    load_in(j + 1)
                for i in range(4):
                    yt = YTB[i]
                    for cg in range(2):
                        p = PS.next()
                        for cc in range(4):
                            c = cg * 4 + cc
                            S.mm(p, p.ap[:, cc * 128:(cc + 1) * 128], yt, yt.ap[:, c * 128:(c + 1) * 128],
                                 m2, m2.ap[:, 0:128], True, False)
                            S.mm(p, p.ap[:, cc * 128:(cc + 1) * 128], yt, yt.ap[:, 1024 + c * 128:1024 + (c + 1) * 128],
                                 m2, m2.ap[:, 128:256], False, True)
                        o_ap = fT_t[:, cg * 4 * T:(cg + 1) * 4 * T].rearrange("p (c q) -> p c q", c=4)[:, :, i * 128:(i + 1) * 128]
                        i_ap = p.ap.rearrange("p (c l) -> p c l", c=4)
                        eng = "act" if cg == 0 else "dve"
                        if eng == "act":
                            S.op("act", lambda e, o=o_ap, a=i_ap: e.activation(out=o, in_=a, func=AF.Copy), [p], fT[cg * 4:(cg + 1) * 4])
                        else:
                            S.op("dve", lambda e, o=o_ap, a=i_ap: e.tensor_copy(out=o, in_=a), [p], fT[cg * 4:(cg + 1) * 4])

                if j + 1 < nt_run:
                    load_yt(j + 1)

                def cons_m(ec, p):
                    S.op("dve", lambda e, o=X[ec].ap, b=p.ap: e.tensor_tensor(out=o, in0=o, in1=b, op=ALU.add), [X[ec], p], [X[ec]])
                    stats(ec)
                proj_fm(GU_WOUT, 4, fT, cons_m)
                ln_finish(1, EPS)
                ffn_down_ln(0, 1, X, 2)
                ffn_down_ln(1, 0, X, 3)
                store_fm(x4T, X_t, X, j)

                def cons_gb(ec, p):
                    if ec % 2 == 0:
                        S.op("act", lambda e, o=GBS[ec].ap, a=p.ap: e.activation(out=o, in_=a, func=AF.Copy), [p], [GBS[ec]])
                    else:
                        S.op("dve", lambda e, o=GBS[ec].ap, a=p.ap: e.tensor_copy(out=o, in_=a), [p], [GBS[ec]])
                proj_fm(GU_CIN, 4, xb, cons_gb)
                store_fm(gbT, gbs_t, GBS, j)
                for q in range(4):
                    wgc = GU.use(GU_CIN + 4 + q)
                    wh = GU.use(GU_CIN + 8 + q)
                    for i in range(4):
                        pgc, ph = PS.next(), PS.next()
                        for dc in range(NDC):
                            S.mm(pgc, pgc.ap[:, 0:256], xb[dc], xb[dc].ap[:, i * 128:(i + 1) * 128],
                                 wgc, wgc.ap[:, dc * 256:(dc + 1) * 256], dc == 0, dc == NDC - 1)
                        for dc in range(NDC):
                            S.mm(ph, ph.ap[:, 0:256], xb[dc], xb[dc].ap[:, i * 128:(i + 1) * 128],
                                 wh, wh.ap[:, dc * 256:(dc + 1) * 256], dc == 0, dc == NDC - 1)
                        gcs, v32 = GCS.next(), V32.next()
                        S.op("act", lambda e, o=gcs.ap, a=pgc.ap[:, 0:256]: e.activation(out=o, in_=a, func=AF.Copy), [pgc], [gcs])
                        S.op("dve", lambda e, o=v32.ap, a=gcs.ap, b=ph.ap[:, 0:256]: e.tensor_tensor(out=o, in0=a, in1=b, op=ALU.mult),
                             [gcs, ph], [v32])
                        for k in range(3):
                            S.op("dve" if k == 0 else "pool", lambda e, o=VST[k][i].ap[:, q * 256:(q + 1) * 256], a=v32.ap,
                                 b=cwb_t[:, k * 1024 + q * 256:k * 1024 + (q + 1) * 256]: e.tensor_tensor(out=o, in0=a, in1=b, op=ALU.mult),
                                 [v32, cwb], [VST[k][i]])
                    GU.done(2)
                for k in range(3):
                    for i in range(4):
                        S.dma("pool", vsc[k, 4 * j + i, :, :], VST[k][i].ap, [VST[k][i]], [])
        S.barrier()

    if "p3b" in phases:
        with ExitStack() as ps4:
            XL_t = sb("XL", NDC * T, F32, ps4)
            XL = chunks(XL_t, NDC, T)
            gbl_t = sb("gbl", NDC * T, BF16, ps4)
            GBL = chunks(gbl_t, NDC, T)
            yT_t = sb("yT", NDC * T, BF16, ps4)
            yT = chunks(yT_t, NDC, T)
            vl_t = [sb("vl%d" % i, 1024, BF16, ps4) for i in range(14)]
            VL = [Buf(t[:, :]) for t in vl_t]
            pt_t = sb("pt", NPAIR * 128, BF16, ps4)
            PT = Buf(pt_t[:, :])
            v0map = {('pt', 2): 0, ('pt', 3): 1, ('own', 0): 2, ('own', 1): 3, ('own', 2): 4}
            v2map = {('own', 1): 5, ('own', 2): 6, ('own', 3): 7, ('nt', 0): 8, ('nt', 1): 9}
            v1map = {0: 10, 1: 11, 2: 12, 3: 13}

            def slot_of(j, where, blk):
                jj = {'own': j, 'pt': (j - 1) % NT, 'nt': (j + 1) % NT}[where]
                return 4 * jj + blk

            def load_in(j):
                load_fm(XL_t, XL, x4T, j)
                S.dma("sp", gbl_t[:, :].rearrange("p (c q) -> p c q", c=NDC),
                      gbT[:, j * T:(j + 1) * T].rearrange("(c p) q -> p c q", p=128), [], GBL)
                S.dma("sp", PT.ap, ptab[:, j * NPAIR * 128:(j + 1) * NPAIR * 128], [], [PT])
                for (wh, blk), bi in v0map.items():
                    S.dma("sp", VL[bi].ap, vsc[0, slot_of(j, wh, blk), :, :], [], [VL[bi]])
                for (wh, blk), bi in v2map.items():
                    S.dma("sp", VL[bi].ap, vsc[2, slot_of(j, wh, blk), :, :], [], [VL[bi]])
                for blk, bi in v1map.items():
                    S.dma("sp", VL[bi].ap, vsc[1, slot_of(j, 'own', blk), :, :], [], [VL[bi]])
            XO_t = sb("XO", NDC * T, F32, ps4)
            XO = chunks(XO_t, NDC, T)

            def conv_mm(c):
                p = PS.next()
                for dest in range(4):
                    lst = []
                    for pi_, (d_, wh, blk) in enumerate(PREV_PAIRS):
                        if d_ == dest:
                            lst.append((VL[v0map[(wh, blk)]], PT, PT.ap[:, pi_ * 128:(pi_ + 1) * 128]))
                    lst.append((VL[v1map[dest]], ident, ident.ap))
                    for pi_, (d_, wh, blk) in enumerate(NEXT_PAIRS):
                        if d_ == dest:
                            lst.append((VL[v2map[(wh, blk)]], PT, PT.ap[:, (9 + pi_) * 128:(10 + pi_) * 128]))
                    for n_, (vb, pb, pap) in enumerate(lst):
                        S.mm(p, p.ap[:, dest * 128:(dest + 1) * 128], vb, vb.ap[:, c * 128:(c + 1) * 128],
                             pb, pap, n_ == 0, n_ == len(lst) - 1)
                return p

            def conv_evac(c, p):
                S.op("dve", lambda e, o=yT[c].ap, a=GBL[c].ap, b=p.ap: e.tensor_tensor(out=o, in0=a, in1=b, op=ALU.mult),
                     [GBL[c], p], [yT[c]])

            def cons_m2(ec, p):
                resid_stats(ec, XL[ec], ALPHA, p)
            load_in(0)
            for c in range(NDC):
                conv_evac(c, conv_mm(c))
            for j in range(nt_run):
                proj_fm(GU_COUT, 4, yT, cons_m2)
                if j + 1 < nt_run:
                    load_in(j + 1)
                ln_finish(4, EPS)
                ffn_down_ln(1, 1, X, 5, finish=False)
                if j + 1 < nt_run:
                    ps_ = [conv_mm(0)]
                    flush_stats()
                    for c in range(1, 6):
                        ps_.append(conv_mm(c))
                    ln_finish(5, 4.0 * EPS, out=XO)
                    for c in range(6):
                        conv_evac(c, ps_[c])
                    for c in range(6, NDC):
                        conv_evac(c, conv_mm(c))
                else:
                    ln_finish(5, 4.0 * EPS, out=XO)
                store_fm(outT, XO_t, XO, j)
    S.barrier(engines=("sp", "pool"))


def slot_maps(kind):
    q = np.arange(NTOK)
    j, i, p = q // 512, (q // 128) % 4, q % 128
    if kind == "prompt":
        pos = 128 * p + 4 * j + i
        return np.zeros(NTOK, int), pos, np.zeros(NTOK, int), pos.copy()
    a, b = i // 2, i % 2
    g, z = p // 64, p % 64
    pos1 = 128 * z + 64 * g + 2 * j + a
    seq1 = b
    a3, g3 = i // 2, i % 2
    b3, z3 = p // 64, p % 64
    pos3 = 2 * j + a3 + 64 * g3 + 128 * z3
    return seq1, pos1, b3, pos3


def dft_tables(kind):
    p = np.arange(128)
    if kind == "prompt":
        slot = np.arange(128)
        rho = np.arange(128)
        ph = ((128 * p[None, :, None] + slot[:, None, None]) * rho[None, None, :]) % 16384
        m1 = np.exp(-2j * np.pi * ph / 16384.0) / np.sqrt(128.0)
        m2 = np.exp(-2j * np.pi * ((slot[:, None] * p[None, :]) % 128) / 128.0) / np.sqrt(128.0)
    else:
        slot = np.arange(128)
        js, al, bs = slot // 4, (slot % 4) // 2, slot % 2
        s2 = 2 * js + al
        g, z = p // 64, p % 64
        s1 = 2 * z + g
        jr, ar, gr = slot // 4, (slot % 4) // 2, slot % 2
        k1 = 2 * jr + ar + 64 * gr
        ph = ((64 * s1[None, :, None] + s2[:, None, None]) * k1[None, None, :]) % 8192
        m1 = np.exp(-2j * np.pi * ph / 8192.0) / np.sqrt(128.0)
        bo, zo = p // 64, p % 64
        m2 = np.exp(-2j * np.pi * ((s2[:, None] * zo[None, :]) % 64) / 64.0) / np.sqrt(64.0)
        m2 = m2 * (bs[:, None] == bo[None, :])
    m1tab = np.stack([m1.real, m1.imag, -m1.imag], axis=2)
    m1tab = np.ascontiguousarray(m1tab.transpose(1, 0, 2, 3)).reshape(128, 128 * 384)
    m2tab = np.concatenate([m2.real, -m2.imag], axis=1)
    return m1tab.astype(bf16), m2tab.astype(bf16)


def conv_tables(kind):
    _, _, seq3, pos3 = slot_maps(kind)
    key = seq3 * 100000 + pos3
    tab = np.zeros((128, NT, NPAIR, 128), np.float32)
    found = np.zeros((NTOK, 2), int)
    for j in range(NT):
        for pi_, (dest, wh, blk) in enumerate(PREV_PAIRS + NEXT_PAIRS):
            delta = -1 if pi_ < 9 else 1
            jj = {'own': j, 'pt': (j - 1) % NT, 'nt': (j + 1) % NT}[wh]
            src = 512 * jj + 128 * blk + np.arange(128)
            dst = 512 * j + 128 * dest + np.arange(128)
            m = (key[src][:, None] == (key[dst] + delta)[None, :]) & (seq3[src][:, None] == seq3[dst][None, :])
            tab[:, j, pi_, :] = m
            found[dst, 0 if delta < 0 else 1] += m.sum(0)
    L = 16384 if kind == "prompt" else 8192
    exp_prev = (pos3 > 0).astype(int)
    exp_next = (pos3 < L - 1).astype(int)
    assert (found[:, 0] == exp_prev).all() and (found[:, 1] == exp_next).all()
    return tab.reshape(128, NT * NPAIR * 128).astype(bf16)


def prep_gu(W):
    n = W.shape[1] // 256
    return np.ascontiguousarray(W.reshape(8, 128, n, 256).transpose(2, 1, 0, 3)).reshape(n, 128, 2048)


def prep_weights(ffn_w_gate, ffn_w_up, ffn_w_down, fnet_w_in, fnet_w_out, conv_w_in, conv_w_out):
    wgu = np.empty((NGU, 128, 2048), np.float32)
    wd = np.empty((ND, 128, 2816), np.float32)
    for l in range(2):
        for k in range(2):
            g = prep_gu(ffn_w_gate[l, k])
            u = prep_gu(ffn_w_up[l, k])
            b = gu_ffn(l, k)
            wgu[b:b + 22:2] = g
            wgu[b + 1:b + 22:2] = u
            wdd = ffn_w_down[l, k].reshape(22, 128, 8, 128).transpose(2, 1, 0, 3).reshape(8, 128, 2816)
            wd[d_ffn(l, k):d_ffn(l, k) + 8] = wdd
    wgu[GU_WIN:GU_WIN + 4] = prep_gu(fnet_w_in[0])
    wgu[GU_WOUT:GU_WOUT + 4] = prep_gu(fnet_w_out[0])
    wgu[GU_CIN:GU_CIN + 12] = prep_gu(conv_w_in[0])
    wgu[GU_COUT:GU_COUT + 4] = prep_gu(conv_w_out[0])
    return wgu, wd


ROLES = [("prompt", [0]), ("sample", [0, 1]), ("sample", [4, None]), ("sample", [5, None]),
         ("prompt", [1]), ("sample", [2, 3]), ("sample", [6, None]), ("sample", [7, None])]

_CACHE = {}


def _tables(kind):
    if kind not in _CACHE:
        m1, m2 = dft_tables(kind)
        _CACHE[kind] = (m1, m2, conv_tables(kind))
    return _CACHE[kind]


def make_in_maps(x_prompt, x_sample, ffn_w_gate, ffn_w_up, ffn_w_down, ln_g, ln_b,
                 fnet_w_in, fnet_w_out, conv_w_in, conv_w, conv_w_out):
    f = lambda a: np.asarray(a, dtype=np.float32)
    x_prompt, x_sample = f(x_prompt), f(x_sample)
    wgu, wd = prep_weights(f(ffn_w_gate), f(ffn_w_up), f(ffn_w_down), f(fnet_w_in), f(fnet_w_out),
                           f(conv_w_in), f(conv_w_out))
    lng = np.ascontiguousarray(f(ln_g).reshape(6, 8, 128).transpose(2, 0, 1)).reshape(128, 48)
    lnb = np.ascontiguousarray(f(ln_b).reshape(6, 8, 128).transpose(2, 0, 1)).reshape(128, 48)
    cwb = np.ascontiguousarray(np.broadcast_to(f(conv_w)[0].reshape(1, 3072), (128, 3072)))
    c = np.arange(128)
    ang = 2 * np.pi * ((c[:, None] * c[None, :]) % 128) / 128.0
    cst = (np.concatenate([np.cos(ang), -np.sin(ang)], axis=1) / np.sqrt(128.0)).astype(bf16)
    ident = np.eye(128, dtype=np.float32).astype(bf16)
    in_maps = []
    for core in range(8):
        kind, ids = ROLES[core]
        seq1, pos1, _, _ = slot_maps(kind)
        if kind == "prompt":
            rows = x_prompt[ids[0]][pos1]
        else:
            seqs = [x_sample[i] if i is not None else np.zeros((8192, D), np.float32) for i in ids]
            rows = np.where((seq1 == 0)[:, None], seqs[0][pos1], seqs[1][pos1])
        m1, m2, pt = _tables(kind)
        bwv = np.zeros((128, 2), np.float32)
        bwv[:, 0 if kind == "prompt" else 1] = ALPHA
        in_maps.append({"xT": np.ascontiguousarray(rows.T), "wgu": wgu, "wd": wd, "m1tab": m1, "m2tab": m2,
                        "cstab": cst, "ptab": pt, "ident": ident, "bw": bwv, "lng": lng, "lnb": lnb, "cwb": cwb})
    return in_maps


def gather_outputs(results):
    y_prompt = np.empty((2, 16384, D), np.float32)
    y_sample = np.empty((8, 8192, D), np.float32)
    for core in range(8):
        kind, ids = ROLES[core]
        _, _, seq3, pos3 = slot_maps(kind)
        o = np.asarray(results[core]["outT"]).T
        if kind == "prompt":
            y_prompt[ids[0]][pos3] = o
        else:
            for b in range(2):
                if ids[b] is not None:
                    m = seq3 == b
                    y_sample[ids[b]][pos3[m]] = o[m]
    return y_prompt, y_sample


_NC = None


def kernel(**inputs):
    global _NC
    in_maps = make_in_maps(**inputs)
    if _NC is None:
        _NC = build_program()
    res = run_bass_kernel_spmd(_NC, in_maps, core_ids=list(range(8)))
    return gather_outputs(res.results)
```

```python
import numpy as np
import ml_dtypes
from contextlib import ExitStack
import concourse.bass as bass
import concourse.mybir as mybir
from concourse.bass_utils import run_bass_kernel_spmd

F32 = mybir.dt.float32
BF16 = mybir.dt.bfloat16
AF = mybir.ActivationFunctionType
ALU = mybir.AluOpType
bf16 = ml_dtypes.bfloat16

D = 1024
DFF = 2816
NDC = 8
NFC = 22
T = 512
NT = 32
NTOK = NT * T
ALPHA = 2.0 ** 0.5
EPS = 1e-5
NGU = 112
ND = 32
GU_WIN, GU_WOUT, GU_CIN, GU_COUT = 88, 92, 96, 108
NPAIR = 18
DBG_STOP = 99
PREV_PAIRS = [(0, 'pt', 2), (0, 'pt', 3), (1, 'pt', 2), (1, 'pt', 3), (1, 'own', 0),
              (2, 'own', 0), (2, 'own', 1), (3, 'own', 1), (3, 'own', 2)]
NEXT_PAIRS = [(0, 'own', 1), (0, 'own', 2), (1, 'own', 2), (1, 'own', 3), (2, 'own', 3),
              (2, 'nt', 0), (2, 'nt', 1), (3, 'nt', 0), (3, 'nt', 1)]


def gu_ffn(l, k):
    return (l * 2 + k) * 22


def d_ffn(l, k):
    return (l * 2 + k) * 8


class Buf:
    __slots__ = ("ap", "w", "r", "const")

    def __init__(self, ap, const=False):
        self.ap = ap
        self.w = None
        self.r = {}
        self.const = const


class Sched:
    KD = 12
    KDQ = {"sp": 12, "pool": 2}

    def __init__(self, nc, stack, dry=False):
        self.nc = nc
        self.dry = dry
        self.eng = {"pe": nc.tensor, "act": nc.scalar, "dve": nc.vector,
                    "pool": nc.gpsimd, "sp": nc.sync}
        self.csem = {}
        self.ccnt = {}
        self.waited = {e: {} for e in self.eng}
        self.dsem = {}
        self.dcnt = {}
        self.dnext = {}
        if not dry:
            for e in ("pe", "act", "dve", "pool"):
                self.csem[e] = stack.enter_context(nc.semaphore("c_" + e))
                self.ccnt[e] = 0
            for q in ("sp", "pool"):
                self.dsem[q] = [stack.enter_context(nc.semaphore("d_%s%d" % (q, i)))
                                for i in range(self.KDQ[q])]
                self.dcnt[q] = [0] * self.KDQ[q]
                self.dnext[q] = 0

    def _wait_key(self, e, key, raw):
        kind, owner, sem, val = key
        if kind == "c" and owner == e:
            if e == "pe" or not raw:
                return
        w = self.waited[e]
        if w.get(id(sem), 0) >= val:
            return
        self.eng[e].wait_ge(sem, val)
        w[id(sem)] = val

    def _deps(self, e, reads, writes):
        for b in reads:
            if b.w is not None:
                self._wait_key(e, b.w, True)
        for b in writes:
            if b.w is not None:
                self._wait_key(e, b.w, True)
            for k in b.r.values():
                self._wait_key(e, k, False)

    def _record(self, key, reads, writes):
        sid = id(key[2])
        for b in reads:
            if not b.const:
                b.r[sid] = key
        for b in writes:
            b.w = key
            b.r = {}

    def op(self, e, fn, reads, writes, inc=True):
        if self.dry:
            return
        self._deps(e, reads, writes)
        ins = fn(self.eng[e])
        if e == "pe" and not inc:
            key = ("c", "pe", self.csem["pe"], self.ccnt["pe"] + 1)
        else:
            self.ccnt[e] += 1
            ins.then_inc(self.csem[e], 1)
            key = ("c", e, self.csem[e], self.ccnt[e])
        self._record(key, reads, writes)

    def dma(self, q, out_ap, in_ap, reads, writes):
        if self.dry:
            return
        self._deps(q, reads, writes)
        k = self.dnext[q]
        self.dnext[q] = (k + 1) % self.KDQ[q]
        sem = self.dsem[q][k]
        if self.dcnt[q][k] > 0:
            w = self.waited[q]
            if w.get(id(sem), 0) < self.dcnt[q][k]:
                self.eng[q].wait_ge(sem, self.dcnt[q][k])
                w[id(sem)] = self.dcnt[q][k]
        self.dcnt[q][k] += 16
        self.eng[q].dma_start(out=out_ap, in_=in_ap).then_inc(sem, 16)
        key = ("d", q, sem, self.dcnt[q][k])
        self._record(key, reads, writes)

    def barrier(self, engines=("sp", "pool")):
        if self.dry:
            return
        for e in engines:
            for q in ("sp", "pool"):
                for k in range(self.KDQ[q]):
                    v = self.dcnt[q][k]
                    if v > 0 and self.waited[e].get(id(self.dsem[q][k]), 0) < v:
                        self.eng[e].wait_ge(self.dsem[q][k], v)
                        self.waited[e][id(self.dsem[q][k])] = v

    def mm(self, out_buf, out_ap, l_buf, l_ap, r_buf, r_ap, start, stop):
        self.op("pe", lambda e: e.matmul(out_ap, l_ap, r_ap, start=start, stop=stop),
                [l_buf, r_buf], [out_buf], inc=stop)


class Rot:
    def __init__(self, bufs):
        self.bufs = bufs
        self.i = 0

    def next(self):
        b = self.bufs[self.i % len(self.bufs)]
        self.i += 1
        return b


class WRing:
    def __init__(self, S, slots, dram, seq):
        self.S = S
        self.slots = slots
        self.dram = dram
        self.seq = seq
        self.pos = 0
        self.issued = 0
        self.consumed = 0
        self.castbufs = None

    def _prefetch(self):
        S = self.S
        R = len(self.slots)
        while self.issued < len(self.seq) and self.issued < self.consumed + R:
            n = self.issued
            sl = self.slots[n % R]
            S.dma("sp", sl.ap, self.dram[self.seq[n]], [self.castbufs[self.seq[n] // 4]], [sl])
            self.issued += 1

    def use(self, pid):
        S = self.S
        if S.dry:
            self.seq.append(pid)
            return self.slots[0]
        assert self.seq[self.pos] == pid, (self.pos, self.seq[self.pos], pid)
        self._prefetch()
        assert self.pos < self.issued, "weight ring: too many pieces held"
        sl = self.slots[self.pos % len(self.slots)]
        self.pos += 1
        return sl

    def done(self, k=1):
        if self.S.dry:
            return
        self.consumed += k
        assert self.consumed <= self.pos
        self._prefetch()


def build_program(nt_run=NT, phases=("p0", "p1", "p3a", "p3b"), dbg=False):
    nc = bass.Bass("TRN2", target_bir_lowering=False)
    dt = nc.dram_tensor
    xT = dt("xT", [D, NTOK], F32, kind="ExternalInput").ap()
    wgu = dt("wgu", [NGU, 128, 2048], F32, kind="ExternalInput").ap()
    wd = dt("wd", [ND, 128, 2816], F32, kind="ExternalInput").ap()
    m1tab = dt("m1tab", [128, 128 * 384], BF16, kind="ExternalInput").ap()
    m2tab = dt("m2tab", [128, 256], BF16, kind="ExternalInput").ap()
    cstab = dt("cstab", [128, 256], BF16, kind="ExternalInput").ap()
    ptab = dt("ptab", [128, NT * NPAIR * 128], BF16, kind="ExternalInput").ap()
    identd = dt("ident", [128, 128], BF16, kind="ExternalInput").ap()
    bwd = dt("bw", [128, 2], F32, kind="ExternalInput").ap()
    lngd = dt("lng", [128, 48], F32, kind="ExternalInput").ap()
    lnbd = dt("lnb", [128, 48], F32, kind="ExternalInput").ap()
    cwbd = dt("cwb", [128, 3072], F32, kind="ExternalInput").ap()
    outT = dt("outT", [D, NTOK], F32, kind="ExternalOutput").ap()
    skind = "ExternalOutput" if dbg else "Internal"
    wgu_b = dt("wgu_b", [NGU, 128, 2048], BF16, kind="Internal").ap()
    wd_b = dt("wd_b", [ND, 128, 2816], BF16, kind="Internal").ap()
    x1T = dt("x1T", [D, NTOK], F32, kind=skind).ap()
    ysc = dt("ysc", [128, 128, 2048], BF16, kind="Internal").ap()
    x4T = dt("x4T", [D, NTOK], F32, kind=skind).ap()
    gbT = dt("gbT", [D, NTOK], BF16, kind="Internal").ap()
    vsc = dt("vsc", [3, 128, 128, 1024], BF16, kind="Internal").ap()

    seqs = {"gu": [], "d": []}
    for dry in (True, False):
        with ExitStack() as st:
            S = Sched(nc, st, dry=dry)
            _emit(nc, st, S, seqs, nt_run, phases, locals())
    return nc


def _emit(nc, st, S, seqs, nt_run, phases, g):
    xT, wgu, wd, m1tab, m2tab, cstab, ptab = g["xT"], g["wgu"], g["wd"], g["m1tab"], g["m2tab"], g["cstab"], g["ptab"]
    identd, bwd, lngd, lnbd, cwbd, outT = g["identd"], g["bwd"], g["lngd"], g["lnbd"], g["cwbd"], g["outT"]
    wgu_b, wd_b, x1T, ysc, x4T, gbT, vsc = g["wgu_b"], g["wd_b"], g["x1T"], g["ysc"], g["x4T"], g["gbT"], g["vsc"]
    dry = S.dry

    def sb(name, cols, dtp, stk=st):
        return stk.enter_context(nc.sbuf_tensor("s_" + name + ("_d" if dry else ""), [128, cols], dtp))

    def chunks(t, n, w):
        return [Buf(t[:, i * w:(i + 1) * w]) for i in range(n)]

    psum = [st.enter_context(nc.psum_tensor("ps%d%s" % (i, "_d" if dry else ""), [128, 512], F32))
            for i in range(8)]
    PS = Rot([Buf(psum[i][:, :]) for i in range(6)])
    S1 = Buf(psum[6][:, :])
    S2 = Buf(psum[7][:, :])
    ones_t = sb("ones", 128, BF16)
    ones = Buf(ones_t[:, :], const=True)
    cs_t = sb("cs", 256, BF16)
    cs = Buf(cs_t[:, :], const=True)
    m2_t = sb("m2", 256, BF16)
    m2 = Buf(m2_t[:, :], const=True)
    id_t = sb("ident", 128, BF16)
    ident = Buf(id_t[:, :], const=True)
    bw_t = sb("bw", 2, F32)
    bw = Buf(bw_t[:, :], const=True)
    lng_t = sb("lng", 48, F32)
    lng = Buf(lng_t[:, :], const=True)
    lnb_t = sb("lnb", 48, F32)
    lnb = Buf(lnb_t[:, :], const=True)
    X_t = sb("X", NDC * T, F32)
    X = chunks(X_t, NDC, T)
    xb_t = sb("xb", NDC * T, BF16)
    xb = chunks(xb_t, NDC, T)
    hT_t = sb("hT", NFC * T, BF16)
    hT = chunks(hT_t, NFC, T)
    sq_t = sb("sq", 3 * T, BF16)
    SQ = Rot(chunks(sq_t, 3, T))
    rb_t = sb("rb", 3 * T, BF16)
    RB = Rot(chunks(rb_t, 3, T))
    sg_t = sb("sg", 2 * T, F32)
    SG = Rot(chunks(sg_t, 2, T))
    mean, var, rstd = [Buf(sb("stat%d" % i, T, F32)[:, :]) for i in range(3)]
    NGUS, NDS = 6, 4
    gus_t = [sb("gus%d" % i, 2048, BF16) for i in range(NGUS)]
    ds_t = [sb("ds%d" % i, 2816, BF16) for i in range(NDS)]
    GU = WRing(S, [Buf(t[:, :]) for t in gus_t], wgu_b, seqs["gu"])
    DR = WRing(S, [Buf(t[:, :]) for t in ds_t], wd_b, seqs["d"])

    S.op("dve", lambda e: e.memset(ones_t[:, :], 1.0), [], [ones])
    dmy_t = sb("dmy", 2, F32)
    dmy = Buf(dmy_t[:, :])
    S.op("dve", lambda e: e.memset(dmy_t[:, :], 1.0), [], [dmy])
    epsc_t = sb("epsc", 2, F32)
    epsc = Buf(epsc_t[:, :], const=True)
    S.op("dve", lambda e: e.memset(epsc_t[:, 0:1], EPS), [], [epsc])
    S.op("dve", lambda e: e.memset(epsc_t[:, 1:2], 4.0 * EPS), [], [epsc])
    S.dma("sp", cs_t[:, :], cstab, [], [cs])
    S.dma("sp", m2_t[:, :], m2tab, [], [m2])
    S.dma("sp", id_t[:, :], identd, [], [ident])
    S.dma("sp", bw_t[:, :], bwd, [], [bw])
    S.dma("sp", lng_t[:, :], lngd, [], [lng])
    S.dma("sp", lnb_t[:, :], lnbd, [], [lnb])

    gu_cast = [Buf(None) for _ in range(NGU // 4)]
    d_cast = [Buf(None) for _ in range(ND // 4)]
    GU.castbufs = gu_cast
    DR.castbufs = d_cast
    first = [("gu", g_) for g_ in range(6)] + [("d", 0), ("d", 1), ("gu", GU_WIN // 4)]
    rest = [("gu", g_) for g_ in range(6, NGU // 4) if g_ != GU_WIN // 4] + [("d", g_) for g_ in range(2, ND // 4)]
    if nt_run < 16:
        first, rest = first + rest, []

    def cast_group(kind, g_):
        if kind == "gu":
            S.dma("pool", wgu_b[4 * g_:4 * g_ + 4], wgu[4 * g_:4 * g_ + 4], [], [gu_cast[g_]])
        else:
            S.dma("pool", wd_b[4 * g_:4 * g_ + 4], wd[4 * g_:4 * g_ + 4], [], [d_cast[g_]])
    if "p0" in phases:
        for kind, g_ in first:
            cast_group(kind, g_)
        if "p1" not in phases:
            for kind, g_ in rest:
                cast_group(kind, g_)
            rest = []

    def ffn(l, k):
        gb_, db_ = gu_ffn(l, k), d_ffn(l, k)

        def evac(fc, pg, pu):
            sg = SG.next()
            S.op("act", lambda e, o=sg.ap, i=pg.ap: e.activation(out=o, in_=i, func=AF.Silu), [pg], [sg])
            S.op("dve", lambda e, o=hT[fc].ap, a=sg.ap, b=pu.ap: e.tensor_tensor(out=o, in0=a, in1=b, op=ALU.mult),
                 [sg, pu], [hT[fc]])

        def wsl(w, dc, cc):
            return w.ap[:, dc * 256 + cc * 128: dc * 256 + cc * 128 + 128]
        w = [GU.use(gb_ + i) for i in range(4)]
        banks = [(PS.next(), PS.next()) for _ in range(3)]
        for dc in range(NDC):
            for fc in range(3):
                pg, pu = banks[fc]
                S.mm(pg, pg.ap, w[2 * (fc // 2)], wsl(w[2 * (fc // 2)], dc, fc % 2), xb[dc], xb[dc].ap, dc == 0, dc == NDC - 1)
                S.mm(pu, pu.ap, w[2 * (fc // 2) + 1], wsl(w[2 * (fc // 2) + 1], dc, fc % 2), xb[dc], xb[dc].ap, dc == 0, dc == NDC - 1)
        GU.done(2)
        for fc in range(3):
            evac(fc, *banks[fc])
        pg, pu = PS.next(), PS.next()
        for dc in range(NDC):
            S.mm(pg, pg.ap, w[2], wsl(w[2], dc, 1), xb[dc], xb[dc].ap, dc == 0, dc == NDC - 1)
        for dc in range(NDC):
            S.mm(pu, pu.ap, w[3], wsl(w[3], dc, 1), xb[dc], xb[dc].ap, dc == 0, dc == NDC - 1)
        GU.done(2)
        evac(3, pg, pu)
        for fp in range(2, 11):
            wg = GU.use(gb_ + 2 * fp)
            wu = GU.use(gb_ + 2 * fp + 1)
            for cc in range(2):
                fc = fp * 2 + cc
                pg, pu = PS.next(), PS.next()
                for dc in range(NDC):
                    S.mm(pg, pg.ap, wg, wsl(wg, dc, cc), xb[dc], xb[dc].ap, dc == 0, dc == NDC - 1)
                for dc in range(NDC):
                    S.mm(pu, pu.ap, wu, wsl(wu, dc, cc), xb[dc], xb[dc].ap, dc == 0, dc == NDC - 1)
                evac(fc, pg, pu)
            GU.done(2)
        return db_

    def resid_stats(c, src, scale, py):
        S.op("dve", lambda e, o=X[c].ap, a=src.ap, b=py.ap: e.scalar_tensor_tensor(
            out=o, in0=a, scalar=float(scale), in1=b, op0=ALU.mult, op1=ALU.add), [src, py], [X[c]])
        stats(c)

    pending = []

    def flush_stats():
        while pending:
            c, sq, rb = pending.pop(0)
            S.mm(S1, S1.ap, ones, ones.ap, rb, rb.ap, c == 0, c == NDC - 1)
            S.mm(S2, S2.ap, ones, ones.ap, sq, sq.ap, c == 0, c == NDC - 1)

    def stats(c):
        sq, rb = SQ.next(), RB.next()
        if c == 0:
            S.op("act", lambda e: e.activation(out=dmy_t[:, 0:1], in_=dmy_t[:, 1:2], func=AF.Ln), [dmy], [dmy])
        S.op("dve", lambda e, o=sq.ap, a=X[c].ap: e.tensor_tensor(out=o, in0=a, in1=a, op=ALU.mult), [X[c]], [sq])
        S.op("act", lambda e, o=rb.ap, a=X[c].ap: e.activation(out=o, in_=a, func=AF.Copy), [X[c]], [rb])
        pending.append((c, sq, rb))

    def ffn_down_ln(l, k, src, lnidx, finish=True):
        db_ = ffn(l, k)
        for c in range(NDC):
            wdd = DR.use(db_ + c)
            py = PS.next()
            for fc in range(NFC):
                S.mm(py, py.ap, wdd, wdd.ap[:, fc * 128:(fc + 1) * 128], hT[fc], hT[fc].ap, fc == 0, fc == NFC - 1)
            DR.done(1)
            flush_stats()
            resid_stats(c, src[c], 2.0 * ALPHA, py)
        if finish:
            ln_finish(lnidx, 4.0 * EPS)

    def ln_finish(lnidx, eps, out=None):
        flush_stats()
        eps_ap = epsc_t[:, 0:1] if eps == EPS else epsc_t[:, 1:2]
        S.op("dve", lambda e: e.tensor_scalar(out=mean.ap, in0=S1.ap, scalar1=1.0 / D, scalar2=None, op0=ALU.mult),
             [S1], [mean])
        S.op("dve", lambda e: e.tensor_tensor(out=var.ap, in0=mean.ap, in1=mean.ap, op=ALU.mult), [mean], [var])
        S.op("dve", lambda e: e.scalar_tensor_tensor(out=var.ap, in0=S2.ap, scalar=1.0 / D, in1=var.ap,
                                                     op0=ALU.mult, op1=ALU.subtract), [S2, var], [var])
        S.op("act", lambda e: e.activation(out=rstd.ap, in_=var.ap, func=AF.Ln, bias=eps_ap, scale=1.0),
             [var, epsc], [rstd])
        S.op("act", lambda e: e.activation(out=rstd.ap, in_=rstd.ap, func=AF.Exp, scale=-0.5), [rstd], [rstd])
        S.op("act", lambda e: e.activation(out=dmy_t[:, 0:1], in_=dmy_t[:, 1:2], func=AF.Silu), [dmy], [dmy])

        def sub(c):
            S.op("dve", lambda e, o=X[c].ap: e.tensor_tensor(out=o, in0=o, in1=mean.ap, op=ALU.subtract), [X[c], mean], [X[c]])

        def mul_cast(c):
            gc_ = lng_t[:, lnidx * 8 + c: lnidx * 8 + c + 1]
            bc_ = lnb_t[:, lnidx * 8 + c: lnidx * 8 + c + 1]
            S.op("dve", lambda e, o=X[c].ap: e.tensor_tensor(out=o, in0=o, in1=rstd.ap, op=ALU.mult), [X[c], rstd], [X[c]])
            if out is None:
                S.op("act", lambda e, o=xb[c].ap, a=X[c].ap, g_=gc_, b_=bc_: e.activation(
                    out=o, in_=a, func=AF.Identity, bias=b_, scale=g_), [X[c], lng, lnb], [xb[c]])
            else:
                S.op("act", lambda e, o=out[c].ap, a=X[c].ap, g_=gc_, b_=bc_: e.activation(
                    out=o, in_=a, func=AF.Identity, bias=b_, scale=g_), [X[c], lng, lnb], [out[c]])
        npool = 0 if lnidx == 1 else 3
        for c in range(NDC - npool, NDC):
            S.op("pool", lambda e, o=X[c].ap: e.tensor_tensor(out=o, in0=o, in1=mean.ap, op=ALU.subtract),
                 [X[c], mean], [X[c]])
        for c in range(3):
            sub(c)
        for c in range(3):
            mul_cast(c)
        for c in range(3, NDC - npool):
            sub(c)
            mul_cast(c)
        for c in range(NDC - npool, NDC):
            mul_cast(c)
        for c in range(NDC if out is None else 0):
            gc_ = lng_t[:, lnidx * 8 + c: lnidx * 8 + c + 1]
            bc_ = lnb_t[:, lnidx * 8 + c: lnidx * 8 + c + 1]
            S.op("act", lambda e, o=X[c].ap, g_=gc_, b_=bc_: e.activation(
                out=o, in_=o, func=AF.Identity, bias=b_, scale=g_), [X[c], lng, lnb], [X[c]])

    def proj_fm(pbase, npieces, rhs, consume, head=False):
        q0 = 0
        if head:
            ws = [GU.use(pbase + q) for q in range(3)]
            banks = [PS.next() for _ in range(6)]
            for dc in range(NDC):
                for ec in range(6):
                    w = ws[ec // 2]
                    cc = ec % 2
                    S.mm(banks[ec], banks[ec].ap, w, w.ap[:, dc * 256 + cc * 128: dc * 256 + cc * 128 + 128],
                         rhs[dc], rhs[dc].ap, dc == 0, dc == NDC - 1)
            GU.done(3)
            for ec in range(6):
                flush_stats()
                consume(ec, banks[ec])
            q0 = 3
        for q in range(q0, npieces):
            w = GU.use(pbase + q)
            for cc in range(2):
                ec = q * 2 + cc
                p = PS.next()
                for dc in range(NDC):
                    S.mm(p, p.ap, w, w.ap[:, dc * 256 + cc * 128: dc * 256 + cc * 128 + 128],
                         rhs[dc], rhs[dc].ap, dc == 0, dc == NDC - 1)
                flush_stats()
                consume(ec, p)
            GU.done(1)

    def load_fm(dst_t, dst_bufs, src, j):
        S.dma("sp", dst_t[:, :].rearrange("p (c q) -> p c q", c=NDC),
              src[:, j * T:(j + 1) * T].rearrange("(c p) q -> p c q", p=128), [], dst_bufs)

    def store_fm(dst, src_t, src_bufs, j):
        S.dma("pool", dst[:, j * T:(j + 1) * T].rearrange("(c p) q -> p c q", p=128),
              src_t[:, :].rearrange("p (c q) -> p c q", c=NDC), src_bufs, [])

    if "p1" in phases:
        with ExitStack() as ps1:
            uT_t = sb("uT", NDC * T, BF16, ps1)
            uT = chunks(uT_t, NDC, T)
            zr_t = [sb("zr%d" % i, 1024, BF16, ps1) for i in range(2)]
            zi_t = [sb("zi%d" % i, 1024, BF16, ps1) for i in range(2)]
            ZR = Rot([Buf(t[:, :]) for t in zr_t])
            ZI = Rot([Buf(t[:, :]) for t in zi_t])
            yb_t = [sb("yb%d" % i, 2048, BF16, ps1) for i in range(2)]
            YB = Rot([Buf(t[:, :]) for t in yb_t])
            m1_t = [sb("m1t%d" % i, 4 * 384, BF16, ps1) for i in range(2)]
            M1 = Rot([Buf(t[:, :]) for t in m1_t])
            X2_t = sb("X2", NDC * T, F32, ps1)
            xsets = [(X_t, list(X)), (X2_t, chunks(X2_t, NDC, T))]
            load_fm(xsets[0][0], xsets[0][1], xT, 0)
            for j in range(nt_run):
                Xc_t, Xc = xsets[j % 2]
                X[:] = Xc
                if j + 1 < nt_run:
                    load_fm(xsets[(j + 1) % 2][0], xsets[(j + 1) % 2][1], xT, j + 1)
                m1 = M1.next()
                S.dma("sp", m1.ap, m1tab[:, j * 1536:(j + 1) * 1536], [], [m1])
                for c in range(NDC):
                    if c % 2 == 0:
                        S.op("act", lambda e, o=xb[c].ap, a=X[c].ap: e.activation(out=o, in_=a, func=AF.Copy), [X[c]], [xb[c]])
                    else:
                        S.op("dve", lambda e, o=xb[c].ap, a=X[c].ap: e.tensor_copy(out=o, in_=a), [X[c]], [xb[c]])
                if DBG_STOP <= 1:
                    continue
                ffn_down_ln(0, 0, X, 0)
                if DBG_STOP <= 2:
                    continue
                store_fm(x1T, Xc_t, X, j)
                if DBG_STOP <= 3:
                    continue

                def cons_u(ec, p):
                    if ec % 2 == 0:
                        S.op("act", lambda e, o=uT[ec].ap, a=p.ap: e.activation(out=o, in_=a, func=AF.Copy), [p], [uT[ec]])
                    else:
                        S.op("dve", lambda e, o=uT[ec].ap, a=p.ap: e.tensor_copy(out=o, in_=a), [p], [uT[ec]])
                proj_fm(GU_WIN, 4, xb, cons_u, head=True)
                if DBG_STOP <= 4:
                    continue
                def z_block(i):
                    zr, zi = ZR.next(), ZI.next()
                    for gp in range(4):
                        pz = PS.next()
                        for gg in range(2):
                            gidx = gp * 2 + gg
                            S.mm(pz, pz.ap[:, gg * 256:(gg + 1) * 256], uT[gidx], uT[gidx].ap[:, i * 128:(i + 1) * 128],
                                 cs, cs.ap, True, True)
                        for gg in range(2):
                            o0 = gp * 256 + gg * 128
                            for (dstb, lo) in ((zr, 0), (zi, 128)):
                                o_ap = dstb.ap[:, o0:o0 + 128]
                                a_ap = pz.ap[:, gg * 256 + lo:gg * 256 + lo + 128]
                                if gp % 2 == 0:
                                    S.op("act", lambda e, o=o_ap, a=a_ap: e.activation(out=o, in_=a, func=AF.Copy), [pz], [dstb])
                                else:
                                    S.op("dve", lambda e, o=o_ap, a=a_ap: e.tensor_copy(out=o, in_=a), [pz], [dstb])
                    return zr, zi

                def s1_block(i, zr, zi):
                    b0 = i * 384
                    m1r, m1i, m1n = m1.ap[:, b0:b0 + 128], m1.ap[:, b0 + 128:b0 + 256], m1.ap[:, b0 + 256:b0 + 384]
                    yb = YB.next()
                    for h in range(2):
                        pr, pi = PS.next(), PS.next()
                        S.mm(pr, pr.ap, m1, m1r, zr, zr.ap[:, h * 512:(h + 1) * 512], True, False)
                        S.mm(pr, pr.ap, m1, m1n, zi, zi.ap[:, h * 512:(h + 1) * 512], False, True)
                        S.mm(pi, pi.ap, m1, m1i, zr, zr.ap[:, h * 512:(h + 1) * 512], True, False)
                        S.mm(pi, pi.ap, m1, m1r, zi, zi.ap[:, h * 512:(h + 1) * 512], False, True)
                        S.op("act", lambda e, o=yb.ap[:, h * 512:(h + 1) * 512], a=pr.ap: e.activation(out=o, in_=a, func=AF.Copy), [pr], [yb])
                        S.op("dve", lambda e, o=yb.ap[:, 1024 + h * 512:1024 + (h + 1) * 512], a=pi.ap: e.tensor_copy(out=o, in_=a), [pi], [yb])
                    S.dma("pool", ysc[:, 4 * j + i, :], yb.ap, [yb], [])
                prev = z_block(0)
                for i in range(1, 4):
                    cur = z_block(i)
                    s1_block(i - 1, *prev)
                    prev = cur
                s1_block(3, *prev)
                for _ in range(2):
                    if rest:
                        cast_group(*rest.pop(0))
            while rest:
                cast_group(*rest.pop(0))
            X[:] = xsets[0][1]
        S.barrier()

    if "p3a" in phases:
        with ExitStack() as ps3:
            XB_t = sb("XB", NDC * T, F32, ps3)
            XB = chunks(XB_t, NDC, T)
            fT_t = sb("fT", NDC * T, BF16, ps3)
            fT = chunks(fT_t, NDC, T)
            yt_t = [sb("yt%d" % i, 2048, BF16, ps3) for i in range(4)]
            YTB = [Buf(t[:, :]) for t in yt_t]

            def load_yt(j):
                for i in range(4):
                    S.dma("sp", YTB[i].ap, ysc[4 * j + i, :, :], [], [YTB[i]])
            gbs_t = sb("gbs", NDC * T, BF16, ps3)
            GBS = chunks(gbs_t, NDC, T)
            gcs_t = sb("gcs", 2 * 256, F32, ps3)
            GCS = Rot(chunks(gcs_t, 2, 256))
            v32_t = sb("v32", 3 * 256, F32, ps3)
            V32 = Rot(chunks(v32_t, 3, 256))
            vst_t = [[sb("vst%d_%d" % (k, i), 1024, BF16, ps3) for i in range(4)] for k in range(3)]
            VST = [[Buf(vst_t[k][i][:, :]) for i in range(4)] for k in range(3)]
            cwb_t = sb("cwb", 3072, F32, ps3)
            cwb = Buf(cwb_t[:, :], const=True)
            S.dma("sp", cwb_t[:, :], cwbd, [], [cwb])

            def load_in(j):
                load_fm(XB_t, XB, x1T, j)
            load_in(0)
            load_yt(0)
            for j in range(nt_run):
                for c in range(NDC):
                    S.op("act", lambda e, o=X[c].ap, i=XB[c].ap: e.activation(out=o, in_=i, func=AF.Copy, scale=float(ALPHA)),
                         [XB[c]], [X[c]])
                    xo3 = X[c].ap.rearrange("p (a q) -> p a q", a=2)
                    xi3 = XB[c].ap.rearrange("p (a q) -> p a q", a=2)
                    S.op("dve", lambda e, o=xo3[:, :, 64:192], i=xi3[:, :, 64:192]: e.tensor_scalar(
                        out=o, in0=i, scalar1=bw_t[:, 0:1], scalar2=None, op0=ALU.mult), [XB[c], X[c], bw], [X[c]])
                    S.op("dve", lambda e, o=xo3[:, :, 64:128], i=xi3[:, :, 128:192]: e.scalar_tensor_tensor(
                        out=o, in0=i, scalar=bw_t[:, 1:2], in1=o, op0=ALU.mult, op1=ALU.add), [XB[c], X[c], bw], [X[c]])
                    S.op("dve", lambda e, o=xo3[:, :, 128:192], i=xi3[:, :, 64:128]: e.scalar_tensor_tensor(
                        out=o, in0=i, scalar=bw_t[:, 1:2], in1=o, op0=ALU.mult, op1=ALU.add), [XB[c], X[c], bw], [X[c]])
                if j + 1 < nt_run:
                    load_in(j + 1)
                for i in range(4):
                    yt = YTB[i]
                    for cg in range(2):
                        p = PS.next()
                        for cc in range(4):
                            c = cg * 4 + cc
                            S.mm(p, p.ap[:, cc * 128:(cc + 1) * 128], yt, yt.ap[:, c * 128:(c + 1) * 128],
                                 m2, m2.ap[:, 0:128], True, False)
                            S.mm(p, p.ap[:, cc * 128:(cc + 1) * 128], yt, yt.ap[:, 1024 + c * 128:1024 + (c + 1) * 128],
                                 m2, m2.ap[:, 128:256], False, True)
                        o_ap = fT_t[:, cg * 4 * T:(cg + 1) * 4 * T].rearrange("p (c q) -> p c q", c=4)[:, :, i * 128:(i + 1) * 128]
                        i_ap = p.ap.rearrange("p (c l) -> p c l", c=4)
                        eng = "act" if cg == 0 else "dve"
                        if eng == "act":
                            S.op("act", lambda e, o=o_ap, a=i_ap: e.activation(out=o, in_=a, func=AF.Copy), [p], fT[cg * 4:(cg + 1) * 4])
                        else:
                            S.op("dve", lambda e, o=o_ap, a=i_ap: e.tensor_copy(out=o, in_=a), [p], fT[cg * 4:(cg + 1) * 4])

                if j + 1 < nt_run:
                    load_yt(j + 1)

                def cons_m(ec, p):
                    S.op("dve", lambda e, o=X[ec].ap, b=p.ap: e.tensor_tensor(out=o, in0=o, in1=b, op=ALU.add), [X[ec], p], [X[ec]])
                    stats(ec)
                proj_fm(GU_WOUT, 4, fT, cons_m)
                ln_finish(1, EPS)
                ffn_down_ln(0, 1, X, 2)
                ffn_down_ln(1, 0, X, 3)
                store_fm(x4T, X_t, X, j)

                def cons_gb(ec, p):
                    if ec % 2 == 0:
                        S.op("act", lambda e, o=GBS[ec].ap, a=p.ap: e.activation(out=o, in_=a, func=AF.Copy), [p], [GBS[ec]])
                    else:
                        S.op("dve", lambda e, o=GBS[ec].ap, a=p.ap: e.tensor_copy(out=o, in_=a), [p], [GBS[ec]])
                proj_fm(GU_CIN, 4, xb, cons_gb, head=True)
                store_fm(gbT, gbs_t, GBS, j)
                for q in range(4):
                    wgc = GU.use(GU_CIN + 4 + q)
                    wh = GU.use(GU_CIN + 8 + q)
                    for i in range(4):
                        pgc, ph = PS.next(), PS.next()
                        for dc in range(NDC):
                            S.mm(pgc, pgc.ap[:, 0:256], xb[dc], xb[dc].ap[:, i * 128:(i + 1) * 128],
                                 wgc, wgc.ap[:, dc * 256:(dc + 1) * 256], dc == 0, dc == NDC - 1)
                        for dc in range(NDC):
                            S.mm(ph, ph.ap[:, 0:256], xb[dc], xb[dc].ap[:, i * 128:(i + 1) * 128],
                                 wh, wh.ap[:, dc * 256:(dc + 1) * 256], dc == 0, dc == NDC - 1)
                        gcs, v32 = GCS.next(), V32.next()
                        S.op("act", lambda e, o=gcs.ap, a=pgc.ap[:, 0:256]: e.activation(out=o, in_=a, func=AF.Copy), [pgc], [gcs])
                        S.op("dve", lambda e, o=v32.ap, a=gcs.ap, b=ph.ap[:, 0:256]: e.tensor_tensor(out=o, in0=a, in1=b, op=ALU.mult),
                             [gcs, ph], [v32])
                        for k in range(3):
                            S.op("dve" if k == 0 else "pool", lambda e, o=VST[k][i].ap[:, q * 256:(q + 1) * 256], a=v32.ap,
                                 b=cwb_t[:, k * 1024 + q * 256:k * 1024 + (q + 1) * 256]: e.tensor_tensor(out=o, in0=a, in1=b, op=ALU.mult),
                                 [v32, cwb], [VST[k][i]])
                    GU.done(2)
                for k in range(3):
                    for i in range(4):
                        S.dma("pool", vsc[k, 4 * j + i, :, :], VST[k][i].ap, [VST[k][i]], [])
        S.barrier()

    if "p3b" in phases:
        with ExitStack() as ps4:
            XL_t = sb("XL", NDC * T, F32, ps4)
            XL = chunks(XL_t, NDC, T)
            gbl_t = sb("gbl", NDC * T, BF16, ps4)
            GBL = chunks(gbl_t, NDC, T)
            yT_t = sb("yT", NDC * T, BF16, ps4)
            yT = chunks(yT_t, NDC, T)
            vl_t = [sb("vl%d" % i, 1024, BF16, ps4) for i in range(14)]
            VL = [Buf(t[:, :]) for t in vl_t]
            pt_t = sb("pt", NPAIR * 128, BF16, ps4)
            PT = Buf(pt_t[:, :])
            v0map = {('pt', 2): 0, ('pt', 3): 1, ('own', 0): 2, ('own', 1): 3, ('own', 2): 4}
            v2map = {('own', 1): 5, ('own', 2): 6, ('own', 3): 7, ('nt', 0): 8, ('nt', 1): 9}
            v1map = {0: 10, 1: 11, 2: 12, 3: 13}

            def slot_of(j, where, blk):
                jj = {'own': j, 'pt': (j - 1) % NT, 'nt': (j + 1) % NT}[where]
                return 4 * jj + blk

            def load_in(j):
                load_fm(XL_t, XL, x4T, j)
                S.dma("sp", gbl_t[:, :].rearrange("p (c q) -> p c q", c=NDC),
                      gbT[:, j * T:(j + 1) * T].rearrange("(c p) q -> p c q", p=128), [], GBL)
                S.dma("sp", PT.ap, ptab[:, j * NPAIR * 128:(j + 1) * NPAIR * 128], [], [PT])
                for (wh, blk), bi in v0map.items():
                    S.dma("sp", VL[bi].ap, vsc[0, slot_of(j, wh, blk), :, :], [], [VL[bi]])
                for (wh, blk), bi in v2map.items():
                    S.dma("sp", VL[bi].ap, vsc[2, slot_of(j, wh, blk), :, :], [], [VL[bi]])
                for blk, bi in v1map.items():
                    S.dma("sp", VL[bi].ap, vsc[1, slot_of(j, 'own', blk), :, :], [], [VL[bi]])
            XO_t = sb("XO", NDC * T, F32, ps4)
            XO = chunks(XO_t, NDC, T)

            def conv_mm(c):
                p = PS.next()
                for dest in range(4):
                    lst = []
                    for pi_, (d_, wh, blk) in enumerate(PREV_PAIRS):
                        if d_ == dest:
                            lst.append((VL[v0map[(wh, blk)]], PT, PT.ap[:, pi_ * 128:(pi_ + 1) * 128]))
                    lst.append((VL[v1map[dest]], ident, ident.ap))
                    for pi_, (d_, wh, blk) in enumerate(NEXT_PAIRS):
                        if d_ == dest:
                            lst.append((VL[v2map[(wh, blk)]], PT, PT.ap[:, (9 + pi_) * 128:(10 + pi_) * 128]))
                    for n_, (vb, pb, pap) in enumerate(lst):
                        S.mm(p, p.ap[:, dest * 128:(dest + 1) * 128], vb, vb.ap[:, c * 128:(c + 1) * 128],
                             pb, pap, n_ == 0, n_ == len(lst) - 1)
                return p

            def conv_evac(c, p):
                S.op("dve", lambda e, o=yT[c].ap, a=GBL[c].ap, b=p.ap: e.tensor_tensor(out=o, in0=a, in1=b, op=ALU.mult),
                     [GBL[c], p], [yT[c]])

            def cons_m2(ec, p):
                resid_stats(ec, XL[ec], ALPHA, p)
            load_in(0)
            for c in range(NDC):
                conv_evac(c, conv_mm(c))
            for j in range(nt_run):
                proj_fm(GU_COUT, 4, yT, cons_m2)
                if j + 1 < nt_run:
                    load_in(j + 1)
                ln_finish(4, EPS)
                ffn_down_ln(1, 1, X, 5, finish=False)
                if j + 1 < nt_run:
                    ps_ = [conv_mm(0)]
                    flush_stats()
                    for c in range(1, 6):
                        ps_.append(conv_mm(c))
                    ln_finish(5, 4.0 * EPS, out=XO)
                    for c in range(6):
                        conv_evac(c, ps_[c])
                    for c in range(6, NDC):
                        conv_evac(c, conv_mm(c))
                else:
                    ln_finish(5, 4.0 * EPS, out=XO)
                store_fm(outT, XO_t, XO, j)
    S.barrier(engines=("sp", "pool"))


def slot_maps(kind):
    q = np.arange(NTOK)
    j, i, p = q // 512, (q // 128) % 4, q % 128
    if kind == "prompt":
        pos = 128 * p + 4 * j + i
        return np.zeros(NTOK, int), pos, np.zeros(NTOK, int), pos.copy()
    a, b = i // 2, i % 2
    g, z = p // 64, p % 64
    pos1 = 128 * z + 64 * g + 2 * j + a
    seq1 = b
    a3, g3 = i // 2, i % 2
    b3, z3 = p // 64, p % 64
    pos3 = 2 * j + a3 + 64 * g3 + 128 * z3
    return seq1, pos1, b3, pos3


def dft_tables(kind):
    p = np.arange(128)
    if kind == "prompt":
        slot = np.arange(128)
        rho = np.arange(128)
        ph = ((128 * p[None, :, None] + slot[:, None, None]) * rho[None, None, :]) % 16384
        m1 = np.exp(-2j * np.pi * ph / 16384.0) / np.sqrt(128.0)
        m2 = np.exp(-2j * np.pi * ((slot[:, None] * p[None, :]) % 128) / 128.0) / np.sqrt(128.0)
    else:
        slot = np.arange(128)
        js, al, bs = slot // 4, (slot % 4) // 2, slot % 2
        s2 = 2 * js + al
        g, z = p // 64, p % 64
        s1 = 2 * z + g
        jr, ar, gr = slot // 4, (slot % 4) // 2, slot % 2
        k1 = 2 * jr + ar + 64 * gr
        ph = ((64 * s1[None, :, None] + s2[:, None, None]) * k1[None, None, :]) % 8192
        m1 = np.exp(-2j * np.pi * ph / 8192.0) / np.sqrt(128.0)
        bo, zo = p // 64, p % 64
        m2 = np.exp(-2j * np.pi * ((s2[:, None] * zo[None, :]) % 64) / 64.0) / np.sqrt(64.0)
        m2 = m2 * (bs[:, None] == bo[None, :])
    m1tab = np.stack([m1.real, m1.imag, -m1.imag], axis=2)
    m1tab = np.ascontiguousarray(m1tab.transpose(1, 0, 2, 3)).reshape(128, 128 * 384)
    m2tab = np.concatenate([m2.real, -m2.imag], axis=1)
    return m1tab.astype(bf16), m2tab.astype(bf16)


def conv_tables(kind):
    _, _, seq3, pos3 = slot_maps(kind)
    key = seq3 * 100000 + pos3
    tab = np.zeros((128, NT, NPAIR, 128), np.float32)
    found = np.zeros((NTOK, 2), int)
    for j in range(NT):
        for pi_, (dest, wh, blk) in enumerate(PREV_PAIRS + NEXT_PAIRS):
            delta = -1 if pi_ < 9 else 1
            jj = {'own': j, 'pt': (j - 1) % NT, 'nt': (j + 1) % NT}[wh]
            src = 512 * jj + 128 * blk + np.arange(128)
            dst = 512 * j + 128 * dest + np.arange(128)
            m = (key[src][:, None] == (key[dst] + delta)[None, :]) & (seq3[src][:, None] == seq3[dst][None, :])
            tab[:, j, pi_, :] = m
            found[dst, 0 if delta < 0 else 1] += m.sum(0)
    L = 16384 if kind == "prompt" else 8192
    exp_prev = (pos3 > 0).astype(int)
    exp_next = (pos3 < L - 1).astype(int)
    assert (found[:, 0] == exp_prev).all() and (found[:, 1] == exp_next).all()
    return tab.reshape(128, NT * NPAIR * 128).astype(bf16)


def prep_gu(W):
    n = W.shape[1] // 256
    return np.ascontiguousarray(W.reshape(8, 128, n, 256).transpose(2, 1, 0, 3)).reshape(n, 128, 2048)


def prep_weights(ffn_w_gate, ffn_w_up, ffn_w_down, fnet_w_in, fnet_w_out, conv_w_in, conv_w_out):
    wgu = np.empty((NGU, 128, 2048), np.float32)
    wd = np.empty((ND, 128, 2816), np.float32)
    for l in range(2):
        for k in range(2):
            g = prep_gu(ffn_w_gate[l, k])
            u = prep_gu(ffn_w_up[l, k])
            b = gu_ffn(l, k)
            wgu[b:b + 22:2] = g
            wgu[b + 1:b + 22:2] = u
            wdd = ffn_w_down[l, k].reshape(22, 128, 8, 128).transpose(2, 1, 0, 3).reshape(8, 128, 2816)
            wd[d_ffn(l, k):d_ffn(l, k) + 8] = wdd
    wgu[GU_WIN:GU_WIN + 4] = prep_gu(fnet_w_in[0])
    wgu[GU_WOUT:GU_WOUT + 4] = prep_gu(fnet_w_out[0])
    wgu[GU_CIN:GU_CIN + 12] = prep_gu(conv_w_in[0])
    wgu[GU_COUT:GU_COUT + 4] = prep_gu(conv_w_out[0])
    return wgu, wd


ROLES = [("prompt", [0]), ("sample", [0, 1]), ("sample", [4, None]), ("sample", [5, None]),
         ("prompt", [1]), ("sample", [2, 3]), ("sample", [6, None]), ("sample", [7, None])]

_CACHE = {}


def _tables(kind):
    if kind not in _CACHE:
        m1, m2 = dft_tables(kind)
        _CACHE[kind] = (m1, m2, conv_tables(kind))
    return _CACHE[kind]


def make_in_maps(x_prompt, x_sample, ffn_w_gate, ffn_w_up, ffn_w_down, ln_g, ln_b,
                 fnet_w_in, fnet_w_out, conv_w_in, conv_w, conv_w_out):
    f = lambda a: np.asarray(a, dtype=np.float32)
    x_prompt, x_sample = f(x_prompt), f(x_sample)
    wgu, wd = prep_weights(f(ffn_w_gate), f(ffn_w_up), f(ffn_w_down), f(fnet_w_in), f(fnet_w_out),
                           f(conv_w_in), f(conv_w_out))
    lng = np.ascontiguousarray(f(ln_g).reshape(6, 8, 128).transpose(2, 0, 1)).reshape(128, 48)
    lnb = np.ascontiguousarray(f(ln_b).reshape(6, 8, 128).transpose(2, 0, 1)).reshape(128, 48)
    cwb = np.ascontiguousarray(np.broadcast_to(f(conv_w)[0].reshape(1, 3072), (128, 3072)))
    c = np.arange(128)
    ang = 2 * np.pi * ((c[:, None] * c[None, :]) % 128) / 128.0
    cst = (np.concatenate([np.cos(ang), -np.sin(ang)], axis=1) / np.sqrt(128.0)).astype(bf16)
    ident = np.eye(128, dtype=np.float32).astype(bf16)
    in_maps = []
    for core in range(8):
        kind, ids = ROLES[core]
        seq1, pos1, _, _ = slot_maps(kind)
        if kind == "prompt":
            rows = x_prompt[ids[0]][pos1]
        else:
            seqs = [x_sample[i] if i is not None else np.zeros((8192, D), np.float32) for i in ids]
            rows = np.where((seq1 == 0)[:, None], seqs[0][pos1], seqs[1][pos1])
        m1, m2, pt = _tables(kind)
        bwv = np.zeros((128, 2), np.float32)
        bwv[:, 0 if kind == "prompt" else 1] = ALPHA
        in_maps.append({"xT": np.ascontiguousarray(rows.T), "wgu": wgu, "wd": wd, "m1tab": m1, "m2tab": m2,
                        "cstab": cst, "ptab": pt, "ident": ident, "bw": bwv, "lng": lng, "lnb": lnb, "cwb": cwb})
    return in_maps


def gather_outputs(results):
    y_prompt = np.empty((2, 16384, D), np.float32)
    y_sample = np.empty((8, 8192, D), np.float32)
    for core in range(8):
        kind, ids = ROLES[core]
        _, _, seq3, pos3 = slot_maps(kind)
        o = np.asarray(results[core]["outT"]).T
        if kind == "prompt":
            y_prompt[ids[0]][pos3] = o
        else:
            for b in range(2):
                if ids[b] is not None:
                    m = seq3 == b
                    y_sample[ids[b]][pos3[m]] = o[m]
    return y_prompt, y_sample


_NC = None


def kernel(**inputs):
    global _NC
    in_maps = make_in_maps(**inputs)
    if _NC is None:
        _NC = build_program()
    res = run_bass_kernel_spmd(_NC, in_maps, core_ids=list(range(8)))
    return gather_outputs(res.results)
```
